# Optimizing a Trainium2 kernel written in Bass

```python
import math
import jax, jax.numpy as jnp
from jax import lax
import numpy as np

D_MODEL = 1024
BATCH = 32
SEQ = 2048
DEPTH = 4

CHUNK = 64
Q_BLOCK = 128
N_MEM = 256
EPS = 1e-6

DA_HEADS = 4
DA_DIM = 64
DA_VDIM = 2 * DA_DIM
SB_HEADS = 4
SB_DIM = 64
MLA_HEADS = 4
MLA_NOPE = 64
MLA_ROPE = 32
MLA_V = 64
MLA_Q_RANK = 256
MLA_KV_RANK = 128
ROPE_THETA = 10000.0
NUM_BUCKETS = 32
MAX_DISTANCE = 128
MEM_HEADS = 4
MEM_DIM = 64
D_FF = 4 * D_MODEL

MIX_WIDTH = DA_HEADS * DA_VDIM + SB_HEADS * SB_DIM + MLA_HEADS * MLA_V
IN_SIZES = (DA_HEADS * 2 * DA_DIM, DA_HEADS * 2 * DA_DIM, DA_HEADS * DA_VDIM,
            SB_HEADS * SB_DIM, SB_HEADS * SB_DIM, SB_HEADS * SB_DIM,
            MLA_Q_RANK, MLA_KV_RANK, MLA_ROPE)
IN_COLS = sum(IN_SIZES)

kernel_name = "hybrid_chunk_causal_diff_sb_mla_trunk"


def rms_norm(x, g):
    xf = x.astype(jnp.float32)
    y = xf * lax.rsqrt(jnp.mean(xf * xf, axis=-1, keepdims=True) + EPS)
    return (y * g.astype(jnp.float32)).astype(x.dtype)


def split_points():
    pts, acc = [], 0
    for n in IN_SIZES[:-1]:
        acc += n
        pts.append(acc)
    return pts


def t5_bucket(rel):
    nb = NUM_BUCKETS // 2
    bucket = (rel > 0).astype(jnp.int32) * nb
    n = jnp.abs(rel)
    max_exact = nb // 2
    is_small = n < max_exact
    large = max_exact + (jnp.log(jnp.maximum(n, 1).astype(jnp.float32) / max_exact)
                         / math.log(MAX_DISTANCE / max_exact) * (nb - max_exact)).astype(jnp.int32)
    large = jnp.minimum(large, nb - 1)
    return bucket + jnp.where(is_small, n, large)


def chunk_mask(q_pos, k_pos):
    return (k_pos[None, :] // CHUNK) <= (q_pos[:, None] // CHUNK)


def rope(x, pos):
    half = MLA_ROPE // 2
    freqs = ROPE_THETA ** (-jnp.arange(half, dtype=jnp.float32) / half)
    ang = pos.astype(jnp.float32)[:, None] * freqs[None, :]
    cos = jnp.cos(ang)[None, :, None, :]
    sin = jnp.sin(ang)[None, :, None, :]
    xf = x.astype(jnp.float32)
    x1, x2 = xf[..., :half], xf[..., half:]
    return jnp.concatenate([x1 * cos - x2 * sin, x2 * cos + x1 * sin], axis=-1).astype(x.dtype)


def over_query_blocks(block_fn, q):
    b, s = q.shape[0], q.shape[1]
    nb = s // Q_BLOCK
    qs = jnp.moveaxis(q.reshape((b, nb, Q_BLOCK) + q.shape[2:]), 1, 0)
    out = lax.map(lambda a: block_fn(a[0], a[1]), (jnp.arange(nb, dtype=jnp.int32), qs))
    out = jnp.moveaxis(out, 0, 1)
    return out.reshape((b, s) + out.shape[3:])


def diff_attention(q, k, v, rel_bias, q_g, k_g, lam_p, subln_g, layer):
    b, s, _ = q.shape
    q = rms_norm(q.reshape(b, s, DA_HEADS, 2, DA_DIM), q_g)
    k = rms_norm(k.reshape(b, s, DA_HEADS, 2, DA_DIM), k_g)
    v = v.reshape(b, s, DA_HEADS, DA_VDIM)
    lam_init = 0.8 - 0.6 * math.exp(-0.3 * layer)
    lp = lam_p.astype(jnp.float32)
    lam = jnp.exp(jnp.sum(lp[0] * lp[1])) - jnp.exp(jnp.sum(lp[2] * lp[3])) + lam_init
    k_pos = jnp.arange(s, dtype=jnp.int32)

    def block(i, qb):
        q_pos = i * Q_BLOCK + jnp.arange(Q_BLOCK, dtype=jnp.int32)
        logits = jnp.einsum('bqhmd,bkhmd->bhmqk', qb, k,
                            preferred_element_type=jnp.float32) * (DA_DIM ** -0.5)
        bias = rel_bias[t5_bucket(k_pos[None, :] - q_pos[:, None])].astype(jnp.float32)
        bias = bias.reshape(Q_BLOCK, s, DA_HEADS, 2).transpose(2, 3, 0, 1)
        logits = jnp.where(chunk_mask(q_pos, k_pos), logits + bias, -jnp.inf)
        p = jax.nn.softmax(logits, axis=-1)
        a = p[:, :, 0] - lam * p[:, :, 1]
        return jnp.einsum('bhqk,bkhe->bqhe', a.astype(v.dtype), v)

    o = over_query_blocks(block, q)
    o = rms_norm(o, subln_g) * (1.0 - lam_init)
    return o.reshape(b, s, DA_HEADS * DA_VDIM)


def stick_breaking(q, k, v, out_g):
    b, s, _ = q.shape
    q = q.reshape(b, s, SB_HEADS, SB_DIM)
    k = k.reshape(b, s, SB_HEADS, SB_DIM)
    v = v.reshape(b, s, SB_HEADS, SB_DIM)
    k_pos = jnp.arange(s, dtype=jnp.int32)

    def block(i, qb):
        q_pos = i * Q_BLOCK + jnp.arange(Q_BLOCK, dtype=jnp.int32)
        z = jnp.einsum('bqhd,bkhd->bhqk', qb, k,
                       preferred_element_type=jnp.float32) * (SB_DIM ** -0.5)
        earlier = k_pos[None, :] < q_pos[:, None]
        log_beta = jax.nn.log_sigmoid(z)
        log_1m_beta = jnp.where(earlier, jax.nn.log_sigmoid(-z), 0.0)
        later = lax.cumsum(log_1m_beta, axis=3, reverse=True) - log_1m_beta
        a = jnp.where(earlier, jnp.exp(log_beta + later), 0.0)
        return jnp.einsum('bhqk,bkhd->bqhd', a.astype(v.dtype), v)

    o = over_query_blocks(block, q)
    return rms_norm(o, out_g).reshape(b, s, SB_HEADS * SB_DIM)


def latent_attention(c_q, c_kv, k_r, pos, cq_g, ckv_g, w_uq, w_ukv, q_g, k_g, out_g):
    b, s, _ = c_q.shape
    q = (rms_norm(c_q, cq_g) @ w_uq).reshape(b, s, MLA_HEADS, MLA_NOPE + MLA_ROPE)
    kv = (rms_norm(c_kv, ckv_g) @ w_ukv).reshape(b, s, MLA_HEADS, MLA_NOPE + MLA_V)
    k_nope, v = kv[..., :MLA_NOPE], kv[..., MLA_NOPE:]
    k = jnp.concatenate([k_nope, jnp.broadcast_to(k_r[:, :, None, :], (b, s, MLA_HEADS, MLA_ROPE))], axis=-1)
    q = rms_norm(q, q_g)
    k = rms_norm(k, k_g)
    q = jnp.concatenate([q[..., :MLA_NOPE], rope(q[..., MLA_NOPE:], pos)], axis=-1)
    k = jnp.concatenate([k[..., :MLA_NOPE], rope(k[..., MLA_NOPE:], pos)], axis=-1)
    scale = (MLA_NOPE + MLA_ROPE) ** -0.5
    k_pos = jnp.arange(s, dtype=jnp.int32)

    def block(i, qb):
        q_pos = i * Q_BLOCK + jnp.arange(Q_BLOCK, dtype=jnp.int32)
        logits = jnp.einsum('bqhd,bkhd->bhqk', qb, k, preferred_element_type=jnp.float32) * scale
        logits = jnp.where(chunk_mask(q_pos, k_pos), logits, -jnp.inf)
        p = jax.nn.softmax(logits, axis=-1)
        return jnp.einsum('bhqk,bkhd->bqhd', p.astype(v.dtype), v)

    o = over_query_blocks(block, q)
    return rms_norm(o, out_g).reshape(b, s, MLA_HEADS * MLA_V)


def memory_attention(h, mem, mem_g, w_q, w_kv, q_g, k_g, w_o):
    b, s, _ = h.shape
    n = mem.shape[1]
    m = rms_norm(mem, mem_g)
    q = rms_norm((h @ w_q).reshape(b, s, MEM_HEADS, MEM_DIM), q_g)
    kv = (m @ w_kv).reshape(b, n, 2, MEM_HEADS, MEM_DIM)
    k = rms_norm(kv[:, :, 0], k_g)
    v = kv[:, :, 1]
    logits = jnp.einsum('bqhd,bkhd->bhqk', q, k, preferred_element_type=jnp.float32) * (MEM_DIM ** -0.5)
    p = jax.nn.softmax(logits, axis=-1)
    o = jnp.einsum('bhqk,bkhd->bqhd', p.astype(v.dtype), v).reshape(b, s, MEM_HEADS * MEM_DIM)
    return o @ w_o


def setup_inputs(seed: int = 0) -> dict:
    key = jax.random.key(seed)
    keys = iter(jax.random.split(key, 40))

    def nrm(shape, scale):
        return jax.random.normal(next(keys), shape, jnp.float32) * scale

    def gain(shape):
        return 1.0 + nrm(shape, 0.02)

    L = DEPTH
    return {
        "x": nrm((BATCH, SEQ, D_MODEL), 1.0),
        "mem": nrm((BATCH, N_MEM, D_MODEL), 1.0),
        "rel_bias": nrm((NUM_BUCKETS, DA_HEADS * 2), 0.2),
        "mix_norm_g": gain((L, D_MODEL)),
        "w_in": nrm((L, D_MODEL, IN_COLS), D_MODEL ** -0.5),
        "da_q_norm_g": gain((L, DA_DIM)),
        "da_k_norm_g": gain((L, DA_DIM)),
        "da_lambda": nrm((L, 4, DA_DIM), 0.1),
        "da_subln_g": gain((L, DA_VDIM)),
        "sb_out_g": gain((L, SB_DIM)),
        "mla_cq_norm_g": gain((L, MLA_Q_RANK)),
        "mla_ckv_norm_g": gain((L, MLA_KV_RANK)),
        "w_mla_uq": nrm((L, MLA_Q_RANK, MLA_HEADS * (MLA_NOPE + MLA_ROPE)), MLA_Q_RANK ** -0.5),
        "w_mla_ukv": nrm((L, MLA_KV_RANK, MLA_HEADS * (MLA_NOPE + MLA_V)), MLA_KV_RANK ** -0.5),
        "mla_q_norm_g": gain((L, MLA_NOPE + MLA_ROPE)),
        "mla_k_norm_g": gain((L, MLA_NOPE + MLA_ROPE)),
        "mla_out_g": gain((L, MLA_V)),
        "w_out": nrm((L, MIX_WIDTH, D_MODEL), 0.5 * MIX_WIDTH ** -0.5),
        "memx_norm_g": gain((L, D_MODEL)),
        "mem_norm_g": gain((L, D_MODEL)),
        "w_mem_q": nrm((L, D_MODEL, MEM_HEADS * MEM_DIM), D_MODEL ** -0.5),
        "w_mem_kv": nrm((L, D_MODEL, 2 * MEM_HEADS * MEM_DIM), D_MODEL ** -0.5),
        "mem_q_norm_g": gain((L, MEM_DIM)),
        "mem_k_norm_g": gain((L, MEM_DIM)),
        "w_mem_o": nrm((L, MEM_HEADS * MEM_DIM, D_MODEL), 0.5 * (MEM_HEADS * MEM_DIM) ** -0.5),
        "ffn_norm_g": gain((L, D_MODEL)),
        "w_ff1": nrm((L, D_MODEL, D_FF), D_MODEL ** -0.5),
        "w_ff2": nrm((L, D_FF, D_MODEL), 0.5 * D_FF ** -0.5),
    }


def reference(x, mem, rel_bias, mix_norm_g, w_in, da_q_norm_g, da_k_norm_g, da_lambda, da_subln_g,
              sb_out_g, mla_cq_norm_g, mla_ckv_norm_g, w_mla_uq, w_mla_ukv, mla_q_norm_g, mla_k_norm_g,
              mla_out_g, w_out, memx_norm_g, mem_norm_g, w_mem_q, w_mem_kv, mem_q_norm_g, mem_k_norm_g,
              w_mem_o, ffn_norm_g, w_ff1, w_ff2):
    s = x.shape[1]
    pos = jnp.arange(s, dtype=jnp.int32)
    pts = split_points()
    for layer in range(DEPTH):
        h = rms_norm(x, mix_norm_g[layer])
        proj = h @ w_in[layer]
        da_q, da_k, da_v, sb_q, sb_k, sb_v, c_q, c_kv, k_r = jnp.split(proj, pts, axis=-1)
        y_a = diff_attention(da_q, da_k, da_v, rel_bias, da_q_norm_g[layer], da_k_norm_g[layer],
                             da_lambda[layer], da_subln_g[layer], layer)
        y_b = stick_breaking(sb_q, sb_k, sb_v, sb_out_g[layer])
        y_c = latent_attention(c_q, c_kv, k_r, pos, mla_cq_norm_g[layer], mla_ckv_norm_g[layer],
                               w_mla_uq[layer], w_mla_ukv[layer], mla_q_norm_g[layer],
                               mla_k_norm_g[layer], mla_out_g[layer])
        y = jnp.concatenate([y_a, y_b, y_c], axis=-1)
        x = x + y @ w_out[layer]
        h = rms_norm(x, memx_norm_g[layer])
        x = x + memory_attention(h, mem, mem_norm_g[layer], w_mem_q[layer], w_mem_kv[layer],
                                 mem_q_norm_g[layer], mem_k_norm_g[layer], w_mem_o[layer])
        h = rms_norm(x, ffn_norm_g[layer])
        x = x + jnp.square(jax.nn.relu(h @ w_ff1[layer])) @ w_ff2[layer]
    return x
```

```python
import math
from contextlib import ExitStack

import numpy as np
import jax
import jax.numpy as jnp
import concourse.bass as bass
import concourse.mybir as mybir
from concourse.bass_utils import run_bass_kernel_spmd

F32 = mybir.dt.float32
BF16 = mybir.dt.bfloat16
AF = mybir.ActivationFunctionType
ALU = mybir.AluOpType
AX = mybir.AxisListType

S = 2048
D = 1024
NM = 256
L = 4
NCORES = 8
NSEQ = 4
EPS = 1e-6
NEG = -30000.0
INC = 2720
GPL = 46
GPN = GPL * L + 8


class V:
    __slots__ = ("ap", "res")

    def __init__(self, ap, *res):
        self.ap = ap
        self.res = res


class Tracker:
    ND = 16

    def __init__(self, nc, es, needed=None):
        self.nc = nc
        self.needed = needed
        self.used = set()
        self.pcnt = {}
        self.phys = {}
        self.E = {"pe": nc.tensor, "act": nc.scalar, "dve": nc.vector,
                  "pool": nc.gpsimd, "sp": nc.sync}
        self.semh = {}
        for e in self.E:
            self.semh[e] = es.enter_context(nc.semaphore("s_" + e))
        for k in range(self.ND):
            self.semh["d%d" % k] = es.enter_context(nc.semaphore("s_d%d" % k))
        self.cnt = {e: 0 for e in self.E}
        self.dcum = [0] * self.ND
        self.dnext_q = {"sp": 0, "pool": 0}
        self.waited = {e: {} for e in self.E}
        self.W = {}
        self.R = {}
        self.out_events = []
        self.ninst = 0
        self.rec = None
        self.vc = {}

    def _deps(self, outs, ins):
        deps = []
        for v in ins:
            for r in v.res:
                w = self.W.get(r)
                if w is not None:
                    deps.append((w[0], w[1], w[2], True))
                if r.startswith("ps"):
                    rd = self.R.get(r)
                    if rd:
                        for sk, (val, src) in rd.items():
                            deps.append((sk, val, src, False))
        for v in outs:
            for r in v.res:
                w = self.W.get(r)
                if w is not None:
                    deps.append((w[0], w[1], w[2], False))
                rd = self.R.get(r)
                if rd:
                    for sk, (val, src) in rd.items():
                        deps.append((sk, val, src, False))
        return deps

    def _wait(self, eng, deps, attach=False):
        wt = self.waited[eng]
        need = {}
        for (sk, val, src, raw) in deps:
            if src == eng and eng == "pe":
                continue
            if wt.get(sk, 0) >= val:
                continue
            if need.get(sk, 0) < val:
                need[sk] = val
        if len(need) > 1:
            for sk, val in list(need.items()):
                if sk not in need:
                    continue
                snap = self.vc.get((sk, val))
                if snap:
                    for s2 in list(need.keys()):
                        if s2 != sk and snap.get(s2, 0) >= need[s2]:
                            del need[s2]
        items = list(need.items())
        last = None
        if attach and ATTACH_WAIT and items:
            last = items.pop()
        for sk, val in items:
            self.E[eng].wait_ge(self.semh[sk], self._pval(sk, val))
        for sk, val in need.items():
            self.used.add((sk, val))
            if wt.get(sk, 0) < val:
                wt[sk] = val
            snap = self.vc.get((sk, val))
            if snap:
                for s2, v2 in snap.items():
                    if wt.get(s2, 0) < v2:
                        wt[s2] = v2
        return last

    def _pval(self, sk, val):
        if sk in self.E:
            return self.phys[(sk, val)]
        return val

    def _record(self, ev, outs, ins):
        sk, val, src = ev
        for v in ins:
            for r in v.res:
                d = self.R.get(r)
                if d is None:
                    d = self.R[r] = {}
                d[sk] = (val, src)
        for v in outs:
            for r in v.res:
                self.W[r] = ev
                self.R[r] = {}

    def record(self, f):
        self.rec = []
        f()
        r = self.rec
        self.rec = None
        return r

    def op(self, eng, fn, outs, ins, inc=True):
        if self.rec is not None:
            self.rec.append((eng, lambda: self._op(eng, fn, outs, ins, inc)))
            return
        self._op(eng, fn, outs, ins, inc)

    def _op(self, eng, fn, outs, ins, inc=True):
        last = self._wait(eng, self._deps(outs, ins), attach=True)
        i = fn(self.E[eng])
        if last is not None:
            i._wait_ge(self.semh[last[0]], self._pval(last[0], last[1]))
        self.ninst += 1
        if inc:
            self.cnt[eng] += 1
            val = self.cnt[eng]
            if self.needed is None or (eng, val) in self.needed:
                i.then_inc(self.semh[eng], 1)
                self.pcnt[eng] = self.pcnt.get(eng, 0) + 1
                self.phys[(eng, val)] = self.pcnt[eng]
        else:
            val = self.cnt[eng] + 1
        if inc:
            snap = dict(self.waited[eng])
            snap[eng] = val
            self.vc[(eng, val)] = snap
        self._record((eng, val, eng), outs, ins)

    def dma_begin(self, q):
        if self.rec is not None:
            g = {}
            self.rec.append(("dma", lambda: g.update(self._dma_begin(q))))
            return g
        return self._dma_begin(q)

    def _dma_begin(self, q):
        base = 0 if q == "sp" else 8
        k = base + self.dnext_q[q] % 8
        self.dnext_q[q] += 1
        sk = "d%d" % k
        if self.waited[q].get(sk, 0) < self.dcum[k]:
            self.E[q].wait_ge(self.semh[sk], self.dcum[k])
            self.waited[q][sk] = self.dcum[k]
        return {"q": q, "k": k, "outs": [], "ins": []}

    def dma(self, g, out_ap, in_ap, outs, ins):
        if self.rec is not None:
            self.rec.append(("dma", lambda: self._dma(g, out_ap, in_ap, outs, ins)))
            return
        self._dma(g, out_ap, in_ap, outs, ins)

    def _dma(self, g, out_ap, in_ap, outs, ins):
        q = g["q"]
        self._wait(q, self._deps(outs, ins))
        kw = {"max_dma_last_dim": 4096} if q == "pool" else {}
        self.E[q].dma_start(out=out_ap, in_=in_ap, **kw).then_inc(self.semh["d%d" % g["k"]], 16)
        self.ninst += 1
        self.dcum[g["k"]] += 16
        g["outs"] += outs
        g["ins"] += ins

    def dma_end(self, g, is_output=False):
        if self.rec is not None:
            self.rec.append(("dma", lambda: self._dma_end(g, is_output)))
            return
        self._dma_end(g, is_output)

    def _dma_end(self, g, is_output=False):
        ev = ("d%d" % g["k"], self.dcum[g["k"]], "dma")
        self.vc[(ev[0], ev[1])] = dict(self.waited[g["q"]])
        self._record(ev, g["outs"], g["ins"])
        if is_output:
            self.out_events.append(ev)

    def finish(self):
        for (sk, val, _) in self.out_events:
            if self.waited["sp"].get(sk, 0) < val:
                self.E["sp"].wait_ge(self.semh[sk], val)
                self.waited["sp"][sk] = val


class BG:
    def __init__(self):
        self.q = []
        self.i = 0
        self.on_empty = []

    def add(self, thunks):
        self.q += thunks

    def _levels_left(self):
        n = 0
        prev = None
        for k in range(self.i, len(self.q)):
            if self.q[k][0] != prev:
                n += 1
                prev = self.q[k][0]
        return n

    def step_for(self, tiles_left):
        if self.i >= len(self.q):
            return
        nlev = -(-self._levels_left() // max(1, tiles_left))
        for _ in range(nlev):
            if self.i >= len(self.q):
                break
            eng = self.q[self.i][0]
            while self.i < len(self.q) and self.q[self.i][0] == eng:
                self.q[self.i][1]()
                self.i += 1

    def pending(self):
        return self.i < len(self.q)

    def flush(self):
        while self.i < len(self.q):
            self.q[self.i][1]()
            self.i += 1
        self.q = []
        self.i = 0
        hooks, self.on_empty = self.on_empty, []
        for h in hooks:
            h()


def build_program(nseq, layers):
    needed = None
    if PRUNE_INCS:
        nc0 = bass.Bass("TRN2", target_bir_lowering=False)
        with ExitStack() as es0:
            needed = _emit(nc0, es0, nseq, layers, None)
    nc = bass.Bass("TRN2", target_bir_lowering=False)
    es = ExitStack()
    with es:
        _emit(nc, es, nseq, layers, needed)
    return nc


def _emit(nc, es, nseq, layers, needed=None):
    def dram(name, shape, kind="ExternalInput"):
        return nc.dram_tensor(name, shape, F32, kind=kind).ap()

    x_d = dram("x", [nseq, S, D])
    mem_d = dram("mem", [nseq, NM, D])
    y_d = dram("y", [nseq, S, D], kind="ExternalOutput")
    w_in_d = dram("w_in", [L, D, INC])
    w_uq_d = dram("w_mla_uq", [L, 256, 384])
    w_ukv_d = dram("w_mla_ukv", [L, 128, 512])
    w_out_d = dram("w_out", [L, D, D])
    w_mq_d = dram("w_mem_q", [L, D, 256])
    w_mkv_d = dram("w_mem_kv", [L, D, 512])
    w_mo_d = dram("w_mem_o", [L, 256, D])
    w_f1_d = dram("w_ff1", [L, D, 4 * D])
    w_f2_d = dram("w_ff2", [L, 4 * D, D])
    gp_d = dram("gp", [128, GPN])
    gs_d = dram("gscale", [128, GPN])
    lamb_d = dram("lamb", [128, L * 256])
    bt_d = dram("bt", [128, 8, 2, 128])
    cst_d = dram("cst", [128, 768])
    rc_d = dram("ropec", [128, S])
    rs_d = dram("ropes", [128, S])

    T = Tracker(nc, es, needed)
    bg = BG()

    def sb(name, shape, dt):
        return es.enter_context(nc.sbuf_tensor(name, shape, dt))

    xT = sb("xT", [128, 8, S], F32)
    hT = sb("hT", [128, 8, S], BF16)
    mhT = sb("mhT", [128, 8, NM], F32)
    mnT = sb("mnT", [128, 8, NM], BF16)
    WB = [sb("WB%d" % i, [128, 8192], BF16) for i in range(2)]
    KT = sb("KT", [128, 2, S], BF16)
    VV = sb("VV", [128, 16, 128], BF16)
    QT = sb("QT", [128, 4, 512], BF16)
    yT = sb("yT", [128, 512], BF16)
    PT = [sb("PT%d" % i, [128, 512], BF16) for i in range(4)]
    FT = [sb("F%d" % i, [128, 512], F32) for i in range(8)]
    SM = [sb("SM%d" % i, [128, 128], F32) for i in range(2)]
    UU = [sb("UU%d" % i, [128, 1024], F32) for i in range(2)]
    cqnT = sb("cqnT", [128, 2, 512], BF16)
    ckvnT = sb("ckvnT", [128, 512], BF16)
    KTm = sb("KTm", [128, 2, NM], BF16)
    Vm = sb("Vm", [128, 2, 256], BF16)
    oTm = sb("oTm", [128, 2, 512], BF16)
    BT = sb("BT", [128, 2, 2, 128], F32)
    CST = sb("CST", [128, 768], F32)
    CSTB = sb("CSTB", [128, 640], BF16)
    SBX = sb("SBX", [128, 1024], F32)
    _sbxb = SBX[:].bitcast(BF16)
    SBH = [_sbxb[:, i * 512:(i + 1) * 512] for i in range(2)]
    SBL = [_sbxb[:, (2 + i) * 512:(3 + i) * 512] for i in range(2)]
    GP = sb("GP", [128, GPN], F32)
    GS = sb("GS", [128, GPN], F32)
    GD = sb("GD", [128, GPN], F32)
    LAM = sb("LAM", [128, 2 * L], F32)
    PS = [es.enter_context(nc.psum_tensor("ps%d" % i, [128, 512], F32)) for i in range(8)]

    def Fv(i, p0=0, p1=128, c0=0, c1=512):
        return V(FT[i][p0:p1, c0:c1], "f%d" % i)

    def PSv(i, p0=0, p1=128, c0=0, c1=512):
        return V(PS[i][p0:p1, c0:c1], "ps%d" % i)

    def PTv(i, p0=0, p1=128, c0=0, c1=512):
        return V(PT[i][p0:p1, c0:c1], "pt%d" % i)

    def xTv(kc, c):
        return V(xT[:, kc, c * 512:(c + 1) * 512], "xT.%d.%d" % (kc, c))

    def hTv(kc, c):
        return V(hT[:, kc, c * 512:(c + 1) * 512], "hT.%d.%d" % (kc, c))

    def hTtok(kc, t):
        return V(hT[:, kc, t * 128:(t + 1) * 128], "hT.%d.%d" % (kc, t // 4))

    IDENT = V(CST[:, 0:128], "cst")
    ONESF = V(CST[:, 128:256], "cst", "@onesf")
    TRI = V(CST[:, 256:384], "cst")
    BO64 = V(CST[:, 384:512], "cst", "@bo64")
    TRIMF = V(CST[:, 512:640], "cst")
    ONESB = V(CSTB[:, 0:128], "cstb")
    ONESB64 = V(CSTB[:, 0:64], "cstb")
    TRIMB = V(CSTB[:, 128:256], "cstb")
    TRIB = V(CSTB[:, 256:384], "cstb")
    TRIPB = V(CSTB[:, 384:512], "cstb")

    def ones_sub(p0, p1, m):
        return V(CST[p0:p1, 128:128 + m], "cst", "@ones:%d:%d:%d" % (p0, p1, m))

    def gd(col, p0=0, p1=128):
        return V(GD[p0:p1, col:col + 1], "gd")

    def _bf16_cols(ap):
        return ap.bitcast(BF16)[:, 0:ap.shape[-1]]

    def mm(out, lhsT, rhs, start, stop, inc=True):
        if BF16_STATS:
            tag = None
            for r in lhsT.res:
                if r.startswith("@"):
                    tag = r
            if tag is not None:
                if tag == "@onesf":
                    lhsT = V(CSTB[:, 0:128], "cstb")
                elif tag == "@bo64":
                    lhsT = V(CSTB[:, 512:640], "cstb")
                else:
                    p0_, p1_, m_ = [int(z) for z in tag.split(":")[1:]]
                    lhsT = V(CSTB[p0_:p1_, 0:m_], "cstb")
                rhs = V(_bf16_cols(rhs.ap), *rhs.res)
        T.op("pe", lambda e: e.matmul(out.ap, lhsT.ap, rhs.ap, start=start, stop=stop, skip_group_check=True),
             [out], [lhsT, rhs], inc=inc)

    def tr(out, in_, inc):
        T.op("pe", lambda e: e.transpose(out.ap, in_.ap, IDENT.ap), [out], [in_, IDENT], inc=True)

    def act(out, in_, func, bias=None, scale=1.0):
        ins = [in_]
        kw = {}
        if isinstance(bias, V):
            ins.append(bias)
            kw["bias"] = bias.ap
        elif bias is not None:
            kw["bias"] = float(bias)
        if scale != 1.0:
            kw["scale"] = float(scale)
        if BF16_STATS and func == AF.Square:
            out = V(_bf16_cols(out.ap), *out.res)
        T.op("act", lambda e: e.activation(out=out.ap, in_=in_.ap, func=func, **kw), [out], ins)

    def tt(eng, out, a, b, op):
        T.op(eng, lambda e: e.tensor_tensor(out.ap, a.ap, b.ap, op), [out], [a, b])

    def ts(eng, out, a, s1, op0):
        ins = [a]
        if isinstance(s1, V):
            ins.append(s1)
            s1v = s1.ap
        else:
            s1v = float(s1)
        T.op(eng, lambda e: e.tensor_scalar(out.ap, a.ap, s1v, None, op0), [out], ins)

    def stt(eng, out, a, s, b, op0, op1):
        ins = [a, b]
        if isinstance(s, V):
            ins.append(s)
            sv = s.ap
        else:
            sv = float(s)
        T.op(eng, lambda e: e.scalar_tensor_tensor(out.ap, a.ap, sv, b.ap, op0, op1), [out], ins)

    def recip(out, a):
        act(out, a, AF.Ln)
        act(out, out, AF.Exp, scale=-1.0)

    def memset(eng, out, val):
        T.op(eng, lambda e: e.memset(out.ap, val), [out], [])

    def rstd_from(dst, src_ps, neps):
        act(dst, src_ps, AF.Ln, bias=neps)
        act(dst, dst, AF.Exp, scale=-0.5)

    g = T.dma_begin("sp")
    T.dma(g, CST[:], cst_d, [V(None, "cst")], [])
    T.dma(g, GP[:], gp_d, [V(None, "gp")], [])
    T.dma(g, GS[:], gs_d, [V(None, "gs")], [])
    T.dma(g, FT[0][:], lamb_d[:, 0:512], [V(None, "f0")], [])
    T.dma(g, FT[1][:], lamb_d[:, 512:1024], [V(None, "f1")], [])
    T.dma_end(g)
    g = T.dma_begin("pool")
    T.dma(g, CSTB[:, 0:128], cst_d[:, 128:256], [V(None, "cstb")], [])
    T.dma(g, CSTB[:, 128:256], cst_d[:, 512:640], [V(None, "cstb")], [])
    T.dma(g, CSTB[:, 256:384], cst_d[:, 256:384], [V(None, "cstb")], [])
    T.dma(g, CSTB[:, 384:512], cst_d[:, 640:768], [V(None, "cstb")], [])
    T.dma(g, CSTB[:, 512:640], cst_d[:, 384:512], [V(None, "cstb")], [])
    T.dma_end(g)
    tt("dve", V(GD[:], "gd"), V(GP[:], "gp"), V(GS[:], "gs"), ALU.mult)
    for l in range(0 if not DEBUG_NOLAM else L, L):
        fsrc = FT[l // 2]
        fres = "f%d" % (l // 2)
        base = (l % 2) * 256
        for j in range(2):
            tt("dve", V(SM[0][:, 0:64], "sm0"), V(fsrc[:, base + j * 128:base + j * 128 + 64], fres),
               V(fsrc[:, base + j * 128 + 64:base + j * 128 + 128], fres), ALU.mult)
            T.op("dve", lambda e, j=j: e.reduce_sum(SM[1][:, j:j + 1], SM[0][:, 0:64], AX.X),
                 [V(None, "sm1")], [V(None, "sm0")])
        act(V(SM[1][:, 0:2], "sm1"), V(SM[1][:, 0:2], "sm1"), AF.Exp)
        lam_init = 0.8 - 0.6 * math.exp(-0.3 * l)
        tt("dve", V(SM[1][:, 2:3], "sm1"), V(SM[1][:, 1:2], "sm1"), V(SM[1][:, 0:1], "sm1"), ALU.subtract)
        ts("dve", V(LAM[:, 2 * l:2 * l + 1], "lam"), V(SM[1][:, 2:3], "sm1"), -lam_init, ALU.add)

    sets = []
    for s in range(nseq):
        for l in layers:
            for h in range(4):
                sets.append(("da", s, l, h))
            for p in range(2):
                sets.append(("sb", s, l, p))
            for p in range(2):
                sets.append(("ml", s, l, p))
            sets.append(("mem", s, l, 0))
            for gidx in range(8):
                sets.append(("ff", s, l, gidx))

    def wview(half, off, n, k=None):
        ap = WB[half][:, off:off + n]
        if k is not None:
            ap = ap.rearrange("p (k n) -> p k n", k=k)
        return ap

    def kc_view(d_ap):
        return d_ap.rearrange("(k p) n -> p k n", p=128)

    def load_set(i):
        kind, s, l, u = sets[i]
        if DEBUG_NOSETS or (DEBUG_SETKINDS is not None and kind not in DEBUG_SETKINDS):
            return
        half = i % 2
        res = [V(None, "wb%d" % half)]
        g = T.dma_begin("pool")

        def ld(off, n, k, src):
            T.dma(g, wview(half, off, n, k), src, res, [])

        win = kc_view(w_in_d[l])
        if kind in ("da", "sb"):
            if kind == "da":
                qc, kcol, vc, orow = u * 128, 512 + u * 128, 1024 + u * 128, u * 128
            else:
                qc, kcol, vc, orow = 1536 + u * 128, 1792 + u * 128, 2048 + u * 128, 512 + u * 128
            ld(0, 1024, 8, win[:, :, qc:qc + 128])
            ld(1024, 1024, 8, win[:, :, kcol:kcol + 128])
            ld(2048, 1024, 8, win[:, :, vc:vc + 128])
            ld(3072, 1024, None, w_out_d[l, orow:orow + 128, :])
        elif kind == "ml":
            ld(0, 2048, 8, win[:, :, 2304:2560])
            ld(2048, 1024, 8, win[:, :, 2560:2688])
            ld(3072, 256, 8, win[:, :, 2688:2720])
            T.dma(g, wview(half, 3328, 256, 8)[:, :, 16:32], win[:, :, 2688:2704], res, [])
            T.dma(g, wview(half, 3328, 256, 8)[:, :, 0:16], win[:, :, 2704:2720], res, [])
            uq = w_uq_d[l].rearrange("(k p) n -> p k n", p=128)
            ld(3584, 384, 2, uq[:, :, u * 192:(u + 1) * 192])
            wp = wview(half, 3968, 128, 2)
            for hh in range(2):
                c96 = (2 * u + hh) * 96
                T.dma(g, wp[:, :, hh * 32:hh * 32 + 16], uq[:, :, c96 + 80:c96 + 96], res, [])
                T.dma(g, wp[:, :, hh * 32 + 16:hh * 32 + 32], uq[:, :, c96 + 64:c96 + 80], res, [])
            ukv = w_ukv_d[l]
            for hh in range(2):
                h = 2 * u + hh
                T.dma(g, wview(half, 4096 + hh * 64, 64), ukv[:, h * 128:h * 128 + 64], res, [])
                T.dma(g, wview(half, 4224 + hh * 64, 64), ukv[:, h * 128 + 64:h * 128 + 128], res, [])
            orow = 768 + u * 128
            ld(4352, 1024, None, w_out_d[l, orow:orow + 128, :])
        elif kind == "mem":
            ld(0, 2048, 8, kc_view(w_mq_d[l]))
            ld(2048, 4096, 8, kc_view(w_mkv_d[l]))
            ld(6144, 2048, 2, w_mo_d[l].rearrange("(k p) n -> p k n", p=128))
        else:
            ld(0, 4096, 8, kc_view(w_f1_d[l])[:, :, u * 512:(u + 1) * 512])
            ld(4096, 4096, 4, w_f2_d[l, u * 512:(u + 1) * 512, :].rearrange("(k p) n -> p k n", p=128))
        T.dma_end(g)

    def xnorm_chunk(gcol, c, temps=None):
        sqa, sqb, rs, pb = temps if temps is not None else ((0, 1, 2, 7), (3, 4, 5, 6))[c % 2]
        for kc in range(8):
            sq = Fv(sqa if kc % 2 == 0 else sqb)
            act(sq, xTv(kc, c), AF.Square)
            mm(PSv(pb), ONESF, sq, kc == 0, kc == 7)
        rstd_from(Fv(rs), PSv(pb), D * EPS)
        for kc in range(8):
            stt("dve", hTv(kc, c), xTv(kc, c), gd(gcol + kc), Fv(rs), ALU.mult, ALU.mult)

    def xnorm(gcol):
        for c in range(4):
            xnorm_chunk(gcol, c)

    def wout_add(wo, rhs_list, c):
        n = len(rhs_list)
        for cb in range(8):
            bank = PSv(6 + cb % 2)
            for k in range(n):
                mm(bank, V(wo[k][:, cb * 128:(cb + 1) * 128], wo_res), rhs_list[k], k == 0, k == n - 1)
            tt("dve", xTv(cb, c), xTv(cb, c), bank, ALU.add)

    wo_res = "wb0"

    def proj8(bank, wv, wres, c, m0=0, m1=128):
        for kc in range(8):
            mm(bank, V(wv[:, kc, :], wres), hTv(kc, c), kc == 0, kc == 7, inc=(kc == 7))

    def groupnorm_to(dst, src_ps, ones_v, neps, gcol, sq=0, rs=1, p0=0, p1=128, n=512, dsts=None):
        act(Fv(sq, p0, p1, 0, n), src_ps, AF.Square)
        mm(PSv(7, p0, p1, 0, n), ones_v, Fv(sq, p0, p1, 0, n), True, True)
        rstd_from(Fv(rs, p0, p1, 0, n), PSv(7, p0, p1, 0, n), neps)
        if dsts is None:
            stt("dve", dst, src_ps, gd(gcol, p0, p1), Fv(rs, p0, p1, 0, n), ALU.mult, ALU.mult)
        else:
            for (dv, a, b) in dsts:
                stt("dve", dv, V(src_ps.ap[a:b, :], *src_ps.res), gd(gcol, a, b), Fv(rs, a, b, 0, n), ALU.mult, ALU.mult)

    def pe_dummy():
        T.op("pe", lambda e: e.matmul(PS[5][:, 0:512], CSTB[:, 0:128], CSTB[:, 0:512], start=True, stop=True,
                                      skip_group_check=True), [], [], inc=False)

    def zero_kt_halves():
        allk0 = ["kt0.%d" % c for c in range(4)]
        allk1 = ["kt1.%d" % c for c in range(4)]
        memset("pool", V(KT[64:128, 0, :], *allk0), 0.0)
        memset("pool", V(KT[0:64, 1, :], *allk1), 0.0)

    def unit_da(i, l, h, defer_last=False):
        half = i % 2
        wres = "wb%d" % half
        nonlocal wo_res
        wo_res = wres
        gb = l * GPL
        wq = wview(half, 0, 1024, 8)
        wk = wview(half, 1024, 1024, 8)
        wv = wview(half, 2048, 1024, 8)
        wo = wview(half, 3072, 1024)
        g = T.dma_begin("sp")
        T.dma(g, BT[:], bt_d[:, 2 * h:2 * h + 2, :, :], [V(None, "bt")], [])
        T.dma_end(g)
        for m_ in range(2):
            ts("dve", V(BT[:, m_, :, :], "bt"), V(BT[:, m_, :, :], "bt"), gd(GPL * L + 2 * h + m_), ALU.subtract)
        memset("pool", V(BT[64:128, :, 0, 0:64], "bt"), NEG)
        zero_kt_halves()

        def prep(c):
            cs = slice(c * 512, (c + 1) * 512)
            sl = c % 2
            proj8(PSv(6), wq, wres, c)
            groupnorm_to(V(QT[:, sl, :], "qt%d" % sl), PSv(6), BO64, 64 * EPS, gb + 32)
            proj8(PSv(6), wk, wres, c)
            groupnorm_to(None, PSv(6), BO64, 64 * EPS, gb + 33,
                         dsts=[(V(KT[0:64, 0, cs], "kt0.%d" % c), 0, 64), (V(KT[64:128, 1, cs], "kt1.%d" % c), 64, 128)])
            for t4 in range(4):
                t = 4 * c + t4
                bank = PSv(6 + t4 % 2, 0, 128, 0, 128)
                for kc in range(8):
                    mm(bank, hTtok(kc, t), V(wv[:, kc, :], wres), kc == 0, kc == 7, inc=(kc == 7))
                act(V(VV[:, t, :], "vv.%d" % t), bank, AF.Copy)

        def prep0():
            c = 0
            cs = slice(0, 512)
            proj8(PSv(6), wq, wres, c)
            proj8(PSv(5), wk, wres, c)
            for t4 in range(4):
                bank = PSv(t4, 0, 128, 0, 128)
                for kc in range(8):
                    mm(bank, hTtok(kc, t4), V(wv[:, kc, :], wres), kc == 0, kc == 7, inc=(kc == 7))
            act(Fv(0), PSv(6), AF.Square)
            act(Fv(6), PSv(5), AF.Square)
            mm(PSv(7), BO64, Fv(0), True, True)
            mm(PSv(4), BO64, Fv(6), True, True)
            for t4 in range(4):
                act(V(VV[:, t4, :], "vv.%d" % t4), PSv(t4, 0, 128, 0, 128), AF.Copy)
            rstd_from(Fv(1), PSv(7), 64 * EPS)
            rstd_from(Fv(7), PSv(4), 64 * EPS)
            stt("dve", V(QT[:, 0, :], "qt0"), PSv(6), gd(gb + 32), Fv(1), ALU.mult, ALU.mult)
            stt("dve", V(KT[0:64, 0, cs], "kt0.0"), PSv(5, 0, 64), gd(gb + 33, 0, 64), Fv(7, 0, 64), ALU.mult, ALU.mult)
            stt("dve", V(KT[64:128, 1, cs], "kt1.0"), PSv(5, 64, 128), gd(gb + 33, 64, 128), Fv(7, 64, 128), ALU.mult, ALU.mult)

        def tiles(c):
            sl = c % 2
            tl = [(m, kb) for m in range(2) for kb in range(4 * c + 4)]
            nkb = 4 * c + 4

            zb = (0, 1, 5) if DUMMY_MM == 0 else (0, 1)
            LA = len(zb) - 1

            def stA(j):
                m, kb = tl[j]
                c0 = max(0, kb - 4 * c) * 128
                mm(PSv(zb[j % len(zb)], 0, 128, c0, 512),
                   V(KT[:, m, kb * 128:(kb + 1) * 128], "kt%d.%d" % (m, kb // 4)),
                   V(QT[:, sl, c0:512], "qt%d" % sl), True, True)

            def stB(j):
                m, kb = tl[j]
                qs0 = max(0, kb - 4 * c)
                cq0 = min(4, max(0, kb - 4 * c + 2))
                bank = zb[j % len(zb)]
                if cq0 > qs0:
                    j0 = 0 if (kb - 4 * c - qs0) == 0 else 1
                    nb = cq0 - qs0
                    T.op("dve", lambda e: e.tensor_tensor(
                        PS[bank][:, qs0 * 128:cq0 * 128].rearrange("p (a b) -> p a b", a=nb),
                        PS[bank][:, qs0 * 128:cq0 * 128].rearrange("p (a b) -> p a b", a=nb),
                        BT[:, m, j0:j0 + nb, :], ALU.add),
                        [V(None, "ps%d" % bank)], [V(None, "ps%d" % bank), V(None, "bt")])
                act(PTv(j % 4, 0, 128, qs0 * 128, 512), PSv(bank, 0, 128, qs0 * 128, 512), AF.Exp)

            def stC(j):
                m, kb = tl[j]
                c0 = max(0, kb - 4 * c) * 128
                mm(PSv(2 + m, 0, 128, c0, 512), V(VV[:, kb, :], "vv.%d" % kb), PTv(j % 4, 0, 128, c0, 512),
                   kb == 0, kb == nkb - 1)
                if DEN_ON_PE:
                    mm(PSv(4, 0, 128, c0, 512), ONESB, PTv(j % 4, 0, 128, c0, 512), kb == 0, kb == nkb - 1)
                    if kb == nkb - 1:
                        recip(Fv(4 + m), PSv(4))
                elif kb == 0:
                    T.op("dve", lambda e, d_=Fv(4 + m), p_=PTv(j % 4): e.tensor_copy(d_.ap, p_.ap), [Fv(4 + m)], [PTv(j % 4)])
                else:
                    tt("dve", Fv(4 + m, 0, 128, c0, 512), Fv(4 + m, 0, 128, c0, 512), PTv(j % 4, 0, 128, c0, 512), ALU.add)
                if kb == nkb - 1 and not DEN_ON_PE:
                    mm(PSv(4), ONESF, Fv(4 + m), True, True)
                    recip(Fv(4 + m), PSv(4))

            n = len(tl)
            for j0 in range(min(LA, n)):
                stA(j0)
            for j in range(n):
                if j + LA < n:
                    stA(j + LA)
                stB(j)
                bg.step_for(n - j)
                stC(j)
                for _ in range(DUMMY_MM):
                    pe_dummy()
            bg.flush()

        def post_a(c):
            tt("dve", Fv(2), PSv(2), Fv(4), ALU.mult)
            tt("dve", Fv(3), PSv(3), Fv(5), ALU.mult)

        def post_b(c):
            stt("dve", Fv(2), Fv(3), V(LAM[:, 2 * l:2 * l + 1], "lam"), Fv(2), ALU.mult, ALU.add)
            act(Fv(0), Fv(2), AF.Square)
            mm(PSv(7), ONESF, Fv(0), True, True)
            rstd_from(Fv(1), PSv(7), 128 * EPS)
            stt("dve", V(yT[:], "yt"), Fv(2), gd(gb + 34), Fv(1), ALU.mult, ALU.mult)
            wout_add([wo], [V(yT[:], "yt")], c)

        run_unit(prep, tiles, post_a, post_b, prep0=prep0, defer_last=defer_last)

    def run_unit(prep, tiles, post_a, post_b_, prep_thunks=None, tail=None, prep0=None, defer_last=False):
        def post_b(c):
            post_b_(c)
            if tail is not None:
                tail(c)

        if prep_thunks is None:
            def prep_thunks(c):
                return T.record(lambda: prep(c))
        if prep0 is not None:
            prep0()
        else:
            for th in prep_thunks(0):
                th[1]()
        for c in range(4):
            if c + 1 < 4:
                bg.add(prep_thunks(c + 1))
            tiles(c)
            post_a(c)
            if PIPELINE and (c < 3 or defer_last):
                bg.add(T.record(lambda: post_b(c)))
            else:
                post_b(c)

    def unit_sb(i, l, p, defer_last=False):
        half = i % 2
        wres = "wb%d" % half
        nonlocal wo_res
        wo_res = wres
        gb = l * GPL
        wq = wview(half, 0, 1024, 8)
        wk = wview(half, 1024, 1024, 8)
        wv = wview(half, 2048, 1024, 8)
        wo = wview(half, 3072, 1024)

        def prep(c):
            cs = slice(c * 512, (c + 1) * 512)
            sl = c % 2
            proj8(PSv(6), wq, wres, c)
            act(V(QT[:, sl, :], "qt%d" % sl), PSv(6), AF.Copy, scale=0.125)
            proj8(PSv(7), wk, wres, c)
            act(V(KT[:, 0, cs], "kt0.%d" % c), PSv(7), AF.Copy)
            for t4 in range(4):
                t = 4 * c + t4
                bank = PSv(6 + t4 % 2, 0, 128, 0, 128)
                for kc in range(8):
                    mm(bank, hTtok(kc, t), V(wv[:, kc, :], wres), kc == 0, kc == 7, inc=(kc == 7))
                act(V(VV[:, t, :], "vv.%d" % t), bank, AF.Copy)

        def prep0():
            c = 0
            proj8(PSv(6), wq, wres, c)
            proj8(PSv(7), wk, wres, c)
            for t4 in range(4):
                bank = PSv(t4, 0, 128, 0, 128)
                for kc in range(8):
                    mm(bank, hTtok(kc, t4), V(wv[:, kc, :], wres), kc == 0, kc == 7, inc=(kc == 7))
            act(V(QT[:, 0, :], "qt0"), PSv(6), AF.Copy, scale=0.125)
            act(V(KT[:, 0, 0:512], "kt0.0"), PSv(7), AF.Copy)
            for t4 in range(4):
                act(V(VV[:, t4, :], "vv.%d" % t4), PSv(t4, 0, 128, 0, 128), AF.Copy)

        def tiles(c):
            sl = c % 2
            nkb = 4 * c + 4
            kbs = list(range(nkb - 1, -1, -1))
            HH = (0, 1)
            tsets = [(0, 1), (2, 3)]
            zb = [0, 1]
            accb = [3, 2]

            def c0_of(kb):
                return max(0, kb - 4 * c) * 128

            def hl(hh, c0):
                return (V(SBH[hh][:, c0:512], "sbh%d" % hh), V(SBL[hh][:, c0:512], "sbl%d" % hh))

            def stA(j, hh):
                kb = kbs[j]
                c0 = c0_of(kb)
                o = hh * 64
                mm(PSv(zb[hh], 0, 128, c0, 512),
                   V(KT[o:o + 64, 0, kb * 128:(kb + 1) * 128], "kt0.%d" % (kb // 4)),
                   V(QT[o:o + 64, sl, c0:512], "qt%d" % sl), True, True)

            def stB1(j, hh):
                kb = kbs[j]
                c0 = c0_of(kb)
                fe, fs = tsets[hh]
                hi, lo = hl(hh, c0)
                act(Fv(fe, 0, 128, c0, 512), PSv(zb[hh], 0, 128, c0, 512), AF.Exp)
                act(Fv(fs, 0, 128, c0, 512), Fv(fe, 0, 128, c0, 512), AF.Ln, bias=1.0)
                if kb >= 4 * c:
                    tt("pool", Fv(fs, 0, 128, c0, c0 + 128), Fv(fs, 0, 128, c0, c0 + 128), TRIMF, ALU.mult)
                T.op("dve", lambda e, h_=hi, s_=Fv(fs, 0, 128, c0, 512): e.tensor_copy(h_.ap, s_.ap), [hi], [Fv(fs)])
                tt("pool", lo, Fv(fs, 0, 128, c0, 512), hi, ALU.subtract)
                tt("dve", Fv(fe, 0, 128, c0, 512), PSv(zb[hh], 0, 128, c0, 512), Fv(fs, 0, 128, c0, 512), ALU.subtract)

            def stC1(j, hh):
                kb = kbs[j]
                c0 = c0_of(kb)
                hi, lo = hl(hh, c0)
                mm(PSv(accb[hh], 0, 128, c0, 512), TRIB, hi, j == 0, False)
                mm(PSv(accb[hh], 0, 128, c0, 512), TRIB, lo, False, False)

            def stB2(j, hh):
                kb = kbs[j]
                c0 = c0_of(kb)
                fe, fs = tsets[hh]
                pt = 2 * hh + j % 2
                tt("dve", Fv(fe, 0, 128, c0, 512), Fv(fe, 0, 128, c0, 512), PSv(accb[hh], 0, 128, c0, 512), ALU.subtract)
                act(PTv(pt, 0, 128, c0, 512), Fv(fe, 0, 128, c0, 512), AF.Exp)
                if kb >= 4 * c:
                    tt("pool", PTv(pt, 0, 128, c0, c0 + 128), PTv(pt, 0, 128, c0, c0 + 128), TRIMB, ALU.mult)

            def stC2(j, hh):
                kb = kbs[j]
                c0 = c0_of(kb)
                o = hh * 64
                hi, lo = hl(hh, c0)
                pt = 2 * hh + j % 2
                mm(PSv(accb[hh], 0, 128, c0, 512), TRIPB, hi, False, False)
                mm(PSv(accb[hh], 0, 128, c0, 512), TRIPB, lo, False, j == nkb - 1)
                mm(PSv(4, o, o + 64, c0, 512), V(VV[:, kb, o:o + 64], "vv.%d" % kb),
                   PTv(pt, 0, 128, c0, 512), j == 0, j == nkb - 1)

            for hh in HH:
                stA(0, hh)
            for hh in HH:
                stB1(0, hh)
            for j in range(nkb):
                for hh in HH:
                    stC1(j, hh)
                bg.step_for(2 * (nkb - j))
                for hh in HH:
                    stB2(j, hh)
                if j + 1 < nkb:
                    for hh in HH:
                        stA(j + 1, hh)
                bg.step_for(2 * (nkb - j) - 1)
                for hh in HH:
                    stC2(j, hh)
                if j + 1 < nkb:
                    for hh in HH:
                        stB1(j + 1, hh)
            bg.flush()

        def post_a(c):
            act(Fv(5), PSv(4), AF.Copy)

        def post_b(c):
            act(Fv(6), Fv(5), AF.Square)
            mm(PSv(7), BO64, Fv(6), True, True)
            rstd_from(Fv(6), PSv(7), 64 * EPS)
            stt("dve", V(yT[:], "yt"), Fv(5), gd(gb + 35), Fv(6), ALU.mult, ALU.mult)
            wout_add([wo], [V(yT[:], "yt")], c)

        run_unit(prep, tiles, post_a, post_b, prep0=prep0, defer_last=defer_last)

    def unit_ml(i, l, p):
        half = i % 2
        wres = "wb%d" % half
        nonlocal wo_res
        wo_res = wres
        gb = l * GPL
        wcq = wview(half, 0, 2048, 8)
        wckv = wview(half, 2048, 1024, 8)
        wkr = wview(half, 3072, 256, 8)
        wkrp = wview(half, 3328, 256, 8)
        wuq = wview(half, 3584, 384, 2)
        wuqp = wview(half, 3968, 128, 2)
        wuk = wview(half, 4096, 128)
        wuv = wview(half, 4224, 128)
        wo = wview(half, 4352, 1024)

        def Gv(i, p0=0, p1=128, c0=0, c1=512):
            buf = UU[i // 2]
            off = (i % 2) * 512
            return V(buf[p0:p1, off + c0:off + c1], "uu%d%s" % (i // 2, "ab"[i % 2]))

        def prep_q(c):
            for j in range(2):
                for kc in range(8):
                    mm(PSv(5 + j), V(wcq[:, kc, j * 128:(j + 1) * 128], wres), hTv(kc, c), kc == 0, kc == 7, inc=(kc == 7))
            act(Fv(0), PSv(5), AF.Square)
            act(Fv(1), PSv(6), AF.Square)
            mm(PSv(7), ONESF, Fv(0), True, False)
            mm(PSv(7), ONESF, Fv(1), False, True)
            rstd_from(Fv(2), PSv(7), 256 * EPS)
            for j in range(2):
                stt("dve", V(cqnT[:, j, :], "cqn"), PSv(5 + j), gd(gb + 36 + j), Fv(2), ALU.mult, ALU.mult)
            for hh in range(2):
                sl = (c % 2) * 2 + hh
                qres = "qt%d" % sl
                for j in range(2):
                    mm(PSv(5, 0, 96), V(wuq[:, j, hh * 96:(hh + 1) * 96], wres), V(cqnT[:, j, :], "cqn"), j == 0, j == 1)
                for j in range(2):
                    mm(PSv(6, 64, 96), V(wuqp[:, j, hh * 32:(hh + 1) * 32], wres), V(cqnT[:, j, :], "cqn"), j == 0, j == 1)
                act(Fv(0, 0, 96), PSv(5, 0, 96), AF.Square)
                mm(PSv(7, 0, 96), ones_sub(0, 96, 96), Fv(0, 0, 96), True, True)
                rstd_from(Fv(1, 0, 96), PSv(7, 0, 96), 96 * EPS)
                stt("dve", V(QT[0:64, sl, :], qres), PSv(5, 0, 64), gd(gb + 39, 0, 64), Fv(1, 0, 64), ALU.mult, ALU.mult)
                stt("dve", Fv(0, 64, 96), PSv(5, 64, 96), gd(gb + 39, 64, 96), Fv(1, 64, 96), ALU.mult, ALU.mult)
                stt("dve", Fv(2, 64, 96), PSv(6, 64, 96), gd(gb + 40, 64, 96), Fv(1, 64, 96), ALU.mult, ALU.mult)
                tt("dve", Fv(0, 64, 96), Fv(0, 64, 96), Fv(3, 64, 96), ALU.mult)
                tt("dve", Fv(2, 64, 96), Fv(2, 64, 96), Fv(4, 64, 96), ALU.mult)
                tt("dve", V(QT[64:96, sl, :], qres), Fv(0, 64, 96), Fv(2, 64, 96), ALU.add)

        def prep_k(c):
            cs = slice(c * 512, (c + 1) * 512)
            g = T.dma_begin("sp")
            T.dma(g, FT[3][64:96, :], rc_d[64:96, cs], [V(None, "f3")], [])
            T.dma(g, FT[4][64:96, :], rs_d[64:96, cs], [V(None, "f4")], [])
            T.dma_end(g)
            proj8(PSv(4), wckv, wres, c)
            act(Gv(0), PSv(4), AF.Square)
            mm(PSv(7), ONESF, Gv(0), True, True)
            rstd_from(Gv(1), PSv(7), 128 * EPS)
            stt("dve", V(ckvnT[:], "ckvn"), PSv(4), gd(gb + 38), Gv(1), ALU.mult, ALU.mult)
            for kc in range(8):
                mm(PSv(4, 64, 96), V(wkr[:, kc, :], wres), hTv(kc, c), kc == 0, kc == 7, inc=(kc == 7))
            act(Fv(5, 64, 96), PSv(4, 64, 96), AF.Copy)
            act(Fv(7, 64, 96), PSv(4, 64, 96), AF.Square)
            for kc in range(8):
                mm(PSv(4, 64, 96), V(wkrp[:, kc, :], wres), hTv(kc, c), kc == 0, kc == 7, inc=(kc == 7))
            act(Fv(6, 64, 96), PSv(4, 64, 96), AF.Copy)
            for t4 in range(4):
                t = 4 * c + t4
                bank = PSv(4, 0, 128, 0, 128)
                mm(bank, V(ckvnT[:, t4 * 128:(t4 + 1) * 128], "ckvn"), V(wuv, wres), True, True)
                act(V(VV[:, t, :], "vv.%d" % t), bank, AF.Copy)
            for hh in range(2):
                mm(PSv(4, 0, 64), V(wuk[:, hh * 64:(hh + 1) * 64], wres), V(ckvnT[:], "ckvn"), True, True)
                act(Gv(0, 0, 64), PSv(4, 0, 64), AF.Square)
                if BF16_STATS:
                    T.op("pool", lambda e: e.tensor_copy(_bf16_cols(UU[0][64:96, 0:512]), _bf16_cols(FT[7][64:96, :])),
                         [Gv(0)], [Fv(7)])
                else:
                    T.op("pool", lambda e: e.tensor_copy(UU[0][64:96, 0:512], FT[7][64:96, :]), [Gv(0)], [Fv(7)])
                mm(PSv(7, 0, 96), ones_sub(0, 96, 96), Gv(0, 0, 96), True, True)
                rstd_from(Gv(1, 0, 96), PSv(7, 0, 96), 96 * EPS)
                ktr = "kt%d.%d" % (hh, c)
                stt("dve", V(KT[0:64, hh, cs], ktr), PSv(4, 0, 64), gd(gb + 41, 0, 64), Gv(1, 0, 64), ALU.mult, ALU.mult)
                stt("dve", Gv(0, 64, 96), Fv(5, 64, 96), gd(gb + 41, 64, 96), Gv(1, 64, 96), ALU.mult, ALU.mult)
                stt("dve", Gv(2, 64, 96), Fv(6, 64, 96), gd(gb + 42, 64, 96), Gv(1, 64, 96), ALU.mult, ALU.mult)
                tt("dve", Gv(0, 64, 96), Gv(0, 64, 96), Fv(3, 64, 96), ALU.mult)
                tt("dve", Gv(2, 64, 96), Gv(2, 64, 96), Fv(4, 64, 96), ALU.mult)
                tt("dve", V(KT[64:96, hh, cs], ktr), Gv(0, 64, 96), Gv(2, 64, 96), ALU.add)

        def prep_thunks(c):
            qa = T.record(lambda: prep_q(c))
            ka = T.record(lambda: prep_k(c))
            out = []
            i = j = 0
            while i < len(qa) or j < len(ka):
                if j >= len(ka) or (i < len(qa) and i * len(ka) <= j * len(qa)):
                    out.append(qa[i])
                    i += 1
                else:
                    out.append(ka[j])
                    j += 1
            return out

        def tiles(c):
            nkb = 4 * c + 4
            for hh in range(2):
                o = hh * 64
                sl = (c % 2) * 2 + hh
                qres = "qt%d" % sl

                def c0_of(kb):
                    return max(0, kb - 4 * c) * 128

                def stA(kb):
                    c0 = c0_of(kb)
                    mm(PSv(kb % 2, 0, 128, c0, 512),
                       V(KT[0:96, hh, kb * 128:(kb + 1) * 128], "kt%d.%d" % (hh, kb // 4)),
                       V(QT[0:96, sl, c0:512], qres), True, True)

                def stB(kb):
                    c0 = c0_of(kb)
                    act(PTv(kb % 4, 0, 128, c0, 512), PSv(kb % 2, 0, 128, c0, 512), AF.Exp)
                    if kb >= 4 * c:
                        memset("pool", PTv(kb % 4, 64, 128, c0, c0 + 64), 0.0)

                def stC(kb):
                    c0 = c0_of(kb)
                    mm(PSv(2, o, o + 64, c0, 512), V(VV[:, kb, o:o + 64], "vv.%d" % kb),
                       PTv(kb % 4, 0, 128, c0, 512), kb == 0, kb == nkb - 1)
                    if DEN_ON_PE and ML_DEN_ON_PE:
                        mm(PSv(3, o, o + 64, c0, 512), ONESB64, PTv(kb % 4, 0, 128, c0, 512), kb == 0, kb == nkb - 1)
                    elif kb == 0:
                        T.op("dve", lambda e, p_=PTv(kb % 4): e.tensor_copy(SBX[:, 0:512], p_.ap),
                             [V(None, "sbh0", "sbh1")], [PTv(kb % 4)])
                    else:
                        tt("dve", V(SBX[:, c0:512], "sbh0", "sbh1"), V(SBX[:, c0:512], "sbh0", "sbh1"),
                           PTv(kb % 4, 0, 128, c0, 512), ALU.add)
                    if kb == nkb - 1 and not (DEN_ON_PE and ML_DEN_ON_PE):
                        mm(PSv(3, o, o + 64), V(CST[:, 128:192], "cst"), V(SBX[:, 0:512], "sbh0", "sbh1"), True, True)

                stA(0)
                for kb in range(nkb):
                    if kb + 1 < nkb:
                        stA(kb + 1)
                    stB(kb)
                    bg.step_for((2 - hh) * nkb - kb)
                    stC(kb)
            bg.flush()

        def post_a(c):
            recip(Fv(0), PSv(3))
            tt("dve", Fv(0), PSv(2), Fv(0), ALU.mult)

        def post_b(c):
            act(Fv(1), Fv(0), AF.Square)
            mm(PSv(7), BO64, Fv(1), True, True)
            rstd_from(Fv(2), PSv(7), 64 * EPS)
            stt("dve", V(yT[:], "yt"), Fv(0), gd(gb + 43), Fv(2), ALU.mult, ALU.mult)
            wout_add([wo], [V(yT[:], "yt")], c)

        tail = None
        if p == 1 and XNORM_PIPE:
            def tail(c):
                xnorm_chunk(gb + 8, c, temps=(0, 1, 2, 7))
        run_unit(None, tiles, post_a, post_b, prep_thunks=prep_thunks, tail=tail)

    def unit_mem(i, l):
        half = i % 2
        wres = "wb%d" % half
        nonlocal wo_res
        wo_res = wres
        gb = l * GPL
        wq = wview(half, 0, 2048, 8)
        wkv = wview(half, 2048, 4096, 8)
        wo = wview(half, 6144, 2048, 2)
        if not XNORM_PIPE:
            xnorm(gb + 8)
        for kc in range(8):
            ts("dve", V(mnT[:, kc, :], "mn"), V(mhT[:, kc, :], "mh"), gd(gb + 24 + kc), ALU.mult)
        for p in range(2):
            for kc in range(8):
                mm(PSv(6, 0, 128, 0, 256), V(wkv[:, kc, p * 128:(p + 1) * 128], wres), V(mnT[:, kc, :], "mn"), kc == 0, kc == 7)
            groupnorm_to(V(KTm[:, p, :], "ktm"), PSv(6, 0, 128, 0, 256), BO64, 64 * EPS, gb + 45, n=256)
        for mb in range(2):
            for kc in range(8):
                mm(PSv(6, 0, 128, 0, 256), V(mnT[:, kc, mb * 128:(mb + 1) * 128], "mn"), V(wkv[:, kc, 256:512], wres), kc == 0, kc == 7)
            act(V(Vm[:, mb, :], "vm"), PSv(6, 0, 128, 0, 256), AF.Copy)
        for c in range(4):
            for p in range(2):
                for kc in range(8):
                    mm(PSv(6), V(wq[:, kc, p * 128:(p + 1) * 128], wres), hTv(kc, c), kc == 0, kc == 7, inc=(kc == 7))
                groupnorm_to(V(QT[:, p, :], "qt%d" % p), PSv(6), BO64, 64 * EPS, gb + 44)
            for p in range(2):
                tl = [(hh, mb) for hh in range(2) for mb in range(2)]

                def stA(j):
                    hh, mb = tl[j]
                    o = hh * 64
                    mm(PSv(j % 2), V(KTm[o:o + 64, p, mb * 128:(mb + 1) * 128], "ktm"),
                       V(QT[o:o + 64, p, :], "qt%d" % p), True, True)

                stA(0)
                for j in range(4):
                    hh, mb = tl[j]
                    o = hh * 64
                    h = 2 * p + hh
                    if j + 1 < 4:
                        stA(j + 1)
                    act(PTv(j % 4), PSv(j % 2), AF.Exp)
                    mm(PSv(2, o, o + 64), V(Vm[:, mb, h * 64:(h + 1) * 64], "vm"), PTv(j % 4), mb == 0, mb == 1)
                    mm(PSv(3, o, o + 64), ONESB64, PTv(j % 4), mb == 0, mb == 1)
                recip(Fv(0), PSv(3))
                tt("dve", V(oTm[:, p, :], "otm%d" % p), PSv(2), Fv(0), ALU.mult)
            wout_add([wo[:, 0, :], wo[:, 1, :]], [V(oTm[:, 0, :], "otm0"), V(oTm[:, 1, :], "otm1")], c)
            if XNORM_PIPE:
                xnorm_chunk(gb + 16, c, temps=(2, 3, 4, 5))

    def unit_ff(i, l, gidx, next_gb=None):
        half = i % 2
        wres = "wb%d" % half
        nonlocal wo_res
        wo_res = wres
        gb = l * GPL
        if gidx == 0 and not XNORM_PIPE:
            xnorm(gb + 16)
        W1 = wview(half, 0, 4096, 8)
        W2 = wview(half, 4096, 4096, 4)

        def ubuf(c):
            ui = (gidx * 4 + c) % 2
            return UU[ui][:].bitcast(BF16).rearrange("p (k n) -> p k n", k=4), ("uu%da" % ui, "uu%db" % ui)

        def ff1(c):
            u, ures = ubuf(c)
            for fc in range(4):
                bank = PSv(fc)
                for kc in range(8):
                    mm(bank, V(W1[:, kc, fc * 128:(fc + 1) * 128], wres), hTv(kc, c), kc == 0, kc == 7, inc=(kc == 7))
                act(Fv(fc), bank, AF.Relu)
                tt("pool", V(u[:, fc, :], *ures), Fv(fc), Fv(fc), ALU.mult)

        def ff2(c):
            u, ures = ubuf(c)
            for cb in range(8):
                bank = PSv(4 + cb % 3)
                for fc in range(4):
                    mm(bank, V(W2[:, fc, cb * 128:(cb + 1) * 128], wres), V(u[:, fc, :], *ures), fc == 0, fc == 3, inc=(fc == 3))
                tt("dve", xTv(cb, c), xTv(cb, c), bank, ALU.add)

        ff1(0)
        for c in range(4):
            if c + 1 < 4:
                ff1(c + 1)
            ff2(c)
            if gidx == 7 and XNORM_PIPE and next_gb is not None:
                xnorm_chunk(next_gb, c, temps=(4, 5, 6, 7))

    si = 0
    load_set(0)
    for s in range(nseq):
        for t in range(16):
            st = UU[t % 2]
            sres = ("uu%da" % (t % 2), "uu%db" % (t % 2))
            g = T.dma_begin("sp")
            T.dma(g, st[:], x_d[s, t * 128:(t + 1) * 128, :], [V(None, *sres)], [])
            T.dma_end(g)
            for hf in range(2):
                bank = PS[6 + hf]
                for j in range(4):
                    kc = 4 * hf + j
                    tr(V(bank[:, j * 128:(j + 1) * 128], "ps%d" % (6 + hf)), V(st[:, kc * 128:(kc + 1) * 128], *sres), j == 3)
                dst = V(xT[:, 4 * hf:4 * hf + 4, t * 128:(t + 1) * 128], *["xT.%d.%d" % (4 * hf + j, t // 4) for j in range(4)])
                src = V(bank[:].rearrange("p (a b) -> p a b", a=4), "ps%d" % (6 + hf))
                if hf == 0:
                    act(dst, src, AF.Copy)
                else:
                    T.op("dve", lambda e, d=dst, s_=src: e.tensor_copy(d.ap, s_.ap), [dst], [src])
        for mt in range(0 if DEBUG_NOMEM else 2):
            st = UU[mt % 2]
            sres = ("uu%da" % (mt % 2), "uu%db" % (mt % 2))
            g = T.dma_begin("sp")
            T.dma(g, st[:], mem_d[s, mt * 128:(mt + 1) * 128, :], [V(None, *sres)], [])
            T.dma_end(g)
            for hf in range(2):
                bank = PS[6 + hf]
                for j in range(4):
                    kc = 4 * hf + j
                    tr(V(bank[:, j * 128:(j + 1) * 128], "ps%d" % (6 + hf)), V(st[:, kc * 128:(kc + 1) * 128], *sres), j == 3)
                dst = V(mhT[:, 4 * hf:4 * hf + 4, mt * 128:(mt + 1) * 128], "mh")
                src = V(bank[:].rearrange("p (a b) -> p a b", a=4), "ps%d" % (6 + hf))
                act(dst, src, AF.Copy)
        for kc in range(0 if DEBUG_NOMEM else 8):
            sq = Fv(kc % 2, 0, 128, 0, 256)
            act(sq, V(mhT[:, kc, :], "mh"), AF.Square)
            mm(PSv(7, 0, 128, 0, 256), ONESF, sq, kc == 0, kc == 7)
        if not DEBUG_NOMEM:
            rstd_from(Fv(2, 0, 128, 0, 256), PSv(7, 0, 128, 0, 256), D * EPS)
        for kc in range(0 if DEBUG_NOMEM else 8):
            tt("dve", V(mhT[:, kc, :], "mh"), V(mhT[:, kc, :], "mh"), Fv(2, 0, 128, 0, 256), ALU.mult)

        for l in layers:
            gb = l * GPL
            li = layers.index(l)
            next_gb = layers[li + 1] * GPL if (li + 1 < len(layers) and XNORM_PIPE and DEBUG_KINDS is None) else None
            if (DEBUG_KINDS is None or len(DEBUG_KINDS) > 0) and (li == 0 or not XNORM_PIPE or DEBUG_KINDS is not None):
                xnorm(gb + 0)
            for k in range(17):
                if si + 1 < len(sets):
                    if bg.pending():
                        bg.on_empty.append(lambda k=si + 1: load_set(k))
                    else:
                        load_set(si + 1)
                kind, _, _, u = sets[si]
                dl = (DEFER_TAIL and DEBUG_KINDS is None and si + 1 < len(sets) and sets[si + 1][0] == kind
                      and kind in ("da", "sb"))
                if DEBUG_KINDS is not None and kind not in DEBUG_KINDS:
                    pass
                elif kind == "da":
                    unit_da(si, l, u, dl)
                elif kind == "sb":
                    unit_sb(si, l, u, dl)
                elif kind == "ml":
                    unit_ml(si, l, u)
                elif kind == "mem":
                    unit_mem(si, l)
                else:
                    unit_ff(si, l, u, next_gb)
                si += 1

        for t in range(16):
            st = UU[t % 2]
            sres = ("uu%da" % (t % 2), "uu%db" % (t % 2))
            for hf in range(2):
                bank = PS[6 + hf]
                for j in range(4):
                    kc = 4 * hf + j
                    tr(V(bank[:, j * 128:(j + 1) * 128], "ps%d" % (6 + hf)),
                       V(xT[:, kc, t * 128:(t + 1) * 128], "xT.%d.%d" % (kc, t // 4)), j == 3)
                dst = V(st[:, hf * 512:(hf + 1) * 512], *sres)
                src = V(bank[:], "ps%d" % (6 + hf))
                if hf == 0:
                    act(dst, src, AF.Copy)
                else:
                    T.op("dve", lambda e, d=dst, s_=src: e.tensor_copy(d.ap, s_.ap), [dst], [src])
            g = T.dma_begin("sp")
            T.dma(g, y_d[s, t * 128:(t + 1) * 128, :], st[:], [], [V(None, *sres)])
            T.dma_end(g, is_output=True)
    T.finish()
    print("instructions:", T.ninst, "logical:", T.cnt, "physical incs:", T.pcnt)
    return T.used


def _t5_bucket(rel_np):
    cpu = jax.devices("cpu")[0]
    with jax.default_device(cpu):
        rel = jnp.asarray(rel_np, dtype=jnp.int32)
        nb = 16
        bucket = (rel > 0).astype(jnp.int32) * nb
        n = jnp.abs(rel)
        max_exact = nb // 2
        is_small = n < max_exact
        large = max_exact + (jnp.log(jnp.maximum(n, 1).astype(jnp.float32) / max_exact)
                             / math.log(128 / max_exact) * (nb - max_exact)).astype(jnp.int32)
        large = jnp.minimum(large, nb - 1)
        out = bucket + jnp.where(is_small, n, large)
        return np.asarray(out)


def _const_tables():
    i = np.arange(128)
    ident = np.eye(128, dtype=np.float32)
    ones = np.ones((128, 128), np.float32)
    tri = (i[:, None] > i[None, :]).astype(np.float32)
    bo64 = ((i[:, None] // 64) == (i[None, :] // 64)).astype(np.float32)
    trim = (i[:, None] < i[None, :]).astype(np.float32)
    trip = (i[:, None] <= i[None, :]).astype(np.float32)
    cst = np.concatenate([ident, ones, tri, bo64, trim, trip], axis=1)
    half = 16
    freqs = (10000.0 ** (-np.arange(half, dtype=np.float32) / half)).astype(np.float32)
    pos = np.arange(S, dtype=np.float32)
    ang = pos[None, :] * freqs[:, None]
    cos = np.cos(ang).astype(np.float32)
    sin = np.sin(ang).astype(np.float32)
    rc = np.zeros((128, S), np.float32)
    rs = np.zeros((128, S), np.float32)
    rc[64:80] = cos
    rc[80:96] = cos
    rs[64:80] = -sin
    rs[80:96] = sin
    gs = np.ones((128, GPN), np.float32)
    for l in range(L):
        b = l * GPL
        lam_init = 0.8 - 0.6 * math.exp(-0.3 * l)
        gs[:, b + 0:b + 32] = 32.0
        gs[:, b + 32] = 1.0
        gs[:, b + 33] = 8.0
        gs[:, b + 34] = math.sqrt(128.0) * (1.0 - lam_init)
        gs[:, b + 35] = 8.0
        gs[:, b + 36:b + 38] = 16.0
        gs[:, b + 38] = math.sqrt(128.0)
        gs[:, b + 39:b + 41] = 1.0
        gs[:, b + 41:b + 43] = math.sqrt(96.0)
        gs[:, b + 43] = 8.0
        gs[:, b + 44] = 1.0
        gs[:, b + 45] = 8.0
    kl = np.arange(128)[:, None, None]
    ql = np.arange(128)[None, None, :]
    jj = np.arange(2)[None, :, None]
    bidx = _t5_bucket(kl - ql - 128 * jj)
    return cst, rc, rs, gs, bidx


def _pack_small(inp):
    p = np.arange(128)
    gp = np.zeros((128, GPN), np.float32)

    def perm96(g):
        out = np.array(g[np.minimum(p, 95)], dtype=np.float32)
        out[64:80] = g[80:96]
        out[80:96] = g[64:80]
        return out

    for l in range(L):
        b = l * GPL
        gp[:, b + 0:b + 8] = inp["mix_norm_g"][l].reshape(8, 128).T
        gp[:, b + 8:b + 16] = inp["memx_norm_g"][l].reshape(8, 128).T
        gp[:, b + 16:b + 24] = inp["ffn_norm_g"][l].reshape(8, 128).T
        gp[:, b + 24:b + 32] = inp["mem_norm_g"][l].reshape(8, 128).T
        gp[:, b + 32] = inp["da_q_norm_g"][l][p % 64]
        gp[:, b + 33] = inp["da_k_norm_g"][l][p % 64]
        gp[:, b + 34] = inp["da_subln_g"][l]
        gp[:, b + 35] = inp["sb_out_g"][l][p % 64]
        gp[:, b + 36:b + 38] = inp["mla_cq_norm_g"][l].reshape(2, 128).T
        gp[:, b + 38] = inp["mla_ckv_norm_g"][l]
        gp[:, b + 39] = inp["mla_q_norm_g"][l][np.minimum(p, 95)]
        gp[:, b + 40] = perm96(inp["mla_q_norm_g"][l])
        gp[:, b + 41] = inp["mla_k_norm_g"][l][np.minimum(p, 95)]
        gp[:, b + 42] = perm96(inp["mla_k_norm_g"][l])
        gp[:, b + 43] = inp["mla_out_g"][l][p % 64]
        gp[:, b + 44] = inp["mem_q_norm_g"][l][p % 64]
        gp[:, b + 45] = inp["mem_k_norm_g"][l][p % 64]
    gp[:, GPL * L:GPL * L + 8] = np.broadcast_to(inp["rel_bias"][15][None, :], (128, 8))
    lamb = np.ascontiguousarray(np.broadcast_to(inp["da_lambda"].reshape(1, L * 256), (128, L * 256)), dtype=np.float32)
    return gp, lamb


_PROGRAMS = {}


def _get_program(nseq, layers):
    key = (nseq, tuple(layers))
    if key not in _PROGRAMS:
        _PROGRAMS[key] = build_program(nseq, list(layers))
    return _PROGRAMS[key]


def _shared_inputs(inp):
    cst, rc, rs, gs, bidx = _const_tables()
    gp, lamb = _pack_small(inp)
    rb = np.asarray(inp["rel_bias"], np.float32)
    bt = np.ascontiguousarray(np.transpose(rb[bidx], (0, 3, 1, 2)))
    shared = {
        "gp": gp, "gscale": gs, "lamb": lamb, "bt": bt, "cst": cst, "ropec": rc, "ropes": rs,
    }
    for k in ("w_in", "w_mla_uq", "w_mla_ukv", "w_out", "w_mem_q", "w_mem_kv", "w_mem_o", "w_ff1", "w_ff2"):
        shared[k] = np.ascontiguousarray(inp[k], dtype=np.float32)
    return shared


FUSED = True
PIPELINE = True
ATTACH_WAIT = True
BF16_STATS = True
DEFER_TAIL = True
PRUNE_INCS = False
DEN_ON_PE = True
ML_DEN_ON_PE = False
DUMMY_MM = 0
XNORM_PIPE = True
DEBUG_KINDS = None
DEBUG_NOSETS = False
DEBUG_SETKINDS = None
DEBUG_NOLAM = False
DEBUG_NOMEM = False


def kernel(**inp):
    x = np.ascontiguousarray(inp["x"], dtype=np.float32)
    mem = np.ascontiguousarray(inp["mem"], dtype=np.float32)
    shared = _shared_inputs(inp)
    xs = [x[c * NSEQ:(c + 1) * NSEQ] for c in range(NCORES)]
    ms = [mem[c * NSEQ:(c + 1) * NSEQ] for c in range(NCORES)]
    launches = [list(range(L))] if FUSED else [[l] for l in range(L)]
    for layers in launches:
        nc = _get_program(NSEQ, layers)
        in_maps = []
        for c in range(NCORES):
            m = dict(shared)
            m["x"] = xs[c]
            m["mem"] = ms[c]
            in_maps.append(m)
        res = run_bass_kernel_spmd(nc, in_maps, core_ids=list(range(NCORES)))
        xs = [np.asarray(res.results[c]["y"], dtype=np.float32) for c in range(NCORES)]
    return np.concatenate(xs, axis=0)
```

```python
import math
from contextlib import ExitStack

import numpy as np
import jax
import jax.numpy as jnp
import concourse.bass as bass
import concourse.mybir as mybir
from concourse.bass_utils import run_bass_kernel_spmd

F32 = mybir.dt.float32
BF16 = mybir.dt.bfloat16
AF = mybir.ActivationFunctionType
ALU = mybir.AluOpType
AX = mybir.AxisListType

S = 2048
D = 1024
NM = 256
L = 4
NCORES = 8
NSEQ = 4
EPS = 1e-6
NEG = -30000.0
INC = 2720
GPL = 46
GPN = GPL * L + 8


class V:
    __slots__ = ("ap", "res")

    def __init__(self, ap, *res):
        self.ap = ap
        self.res = res


class Tracker:
    ND = 16

    def __init__(self, nc, es, needed=None):
        self.nc = nc
        self.needed = needed
        self.used = set()
        self.pcnt = {}
        self.phys = {}
        self.E = {"pe": nc.tensor, "act": nc.scalar, "dve": nc.vector,
                  "pool": nc.gpsimd, "sp": nc.sync}
        self.semh = {}
        for e in self.E:
            self.semh[e] = es.enter_context(nc.semaphore("s_" + e))
        for k in range(self.ND):
            self.semh["d%d" % k] = es.enter_context(nc.semaphore("s_d%d" % k))
        self.cnt = {e: 0 for e in self.E}
        self.dcum = [0] * self.ND
        self.dnext_q = {"sp": 0, "pool": 0}
        self.waited = {e: {} for e in self.E}
        self.W = {}
        self.R = {}
        self.out_events = []
        self.ninst = 0
        self.rec = None
        self.vc = {}

    def _deps(self, outs, ins):
        deps = []
        for v in ins:
            for r in v.res:
                w = self.W.get(r)
                if w is not None:
                    deps.append((w[0], w[1], w[2], True))
                if r.startswith("ps"):
                    rd = self.R.get(r)
                    if rd:
                        for sk, (val, src) in rd.items():
                            deps.append((sk, val, src, False))
        for v in outs:
            for r in v.res:
                w = self.W.get(r)
                if w is not None:
                    deps.append((w[0], w[1], w[2], False))
                rd = self.R.get(r)
                if rd:
                    for sk, (val, src) in rd.items():
                        deps.append((sk, val, src, False))
        return deps

    def _wait(self, eng, deps, attach=False):
        wt = self.waited[eng]
        need = {}
        for (sk, val, src, raw) in deps:
            if src == eng and eng == "pe":
                continue
            if wt.get(sk, 0) >= val:
                continue
            if need.get(sk, 0) < val:
                need[sk] = val
        if len(need) > 1:
            for sk, val in list(need.items()):
                if sk not in need:
                    continue
                snap = self.vc.get((sk, val))
                if snap:
                    for s2 in list(need.keys()):
                        if s2 != sk and snap.get(s2, 0) >= need[s2]:
                            del need[s2]
        items = list(need.items())
        last = None
        if attach and ATTACH_WAIT and items:
            last = items.pop()
        for sk, val in items:
            self.E[eng].wait_ge(self.semh[sk], self._pval(sk, val))
        for sk, val in need.items():
            self.used.add((sk, val))
            if wt.get(sk, 0) < val:
                wt[sk] = val
            snap = self.vc.get((sk, val))
            if snap:
                for s2, v2 in snap.items():
                    if wt.get(s2, 0) < v2:
                        wt[s2] = v2
        return last

    def _pval(self, sk, val):
        if sk in self.E:
            return self.phys[(sk, val)]
        return val

    def _record(self, ev, outs, ins):
        sk, val, src = ev
        for v in ins:
            for r in v.res:
                d = self.R.get(r)
                if d is None:
                    d = self.R[r] = {}
                d[sk] = (val, src)
        for v in outs:
            for r in v.res:
                self.W[r] = ev
                self.R[r] = {}

    def record(self, f):
        self.rec = []
        f()
        r = self.rec
        self.rec = None
        return r

    def op(self, eng, fn, outs, ins, inc=True):
        if self.rec is not None:
            self.rec.append((eng, lambda: self._op(eng, fn, outs, ins, inc)))
            return
        self._op(eng, fn, outs, ins, inc)

    def _op(self, eng, fn, outs, ins, inc=True):
        last = self._wait(eng, self._deps(outs, ins), attach=True)
        i = fn(self.E[eng])
        if last is not None:
            i._wait_ge(self.semh[last[0]], self._pval(last[0], last[1]))
        self.ninst += 1
        if inc:
            self.cnt[eng] += 1
            val = self.cnt[eng]
            if self.needed is None or (eng, val) in self.needed:
                i.then_inc(self.semh[eng], 1)
                self.pcnt[eng] = self.pcnt.get(eng, 0) + 1
                self.phys[(eng, val)] = self.pcnt[eng]
        else:
            val = self.cnt[eng] + 1
        if inc:
            snap = dict(self.waited[eng])
            snap[eng] = val
            self.vc[(eng, val)] = snap
        self._record((eng, val, eng), outs, ins)

    def dma_begin(self, q):
        if self.rec is not None:
            g = {}
            self.rec.append(("dma", lambda: g.update(self._dma_begin(q))))
            return g
        return self._dma_begin(q)

    def _dma_begin(self, q):
        base = 0 if q == "sp" else 8
        k = base + self.dnext_q[q] % 8
        self.dnext_q[q] += 1
        sk = "d%d" % k
        if self.waited[q].get(sk, 0) < self.dcum[k]:
            self.E[q].wait_ge(self.semh[sk], self.dcum[k])
            self.waited[q][sk] = self.dcum[k]
        return {"q": q, "k": k, "outs": [], "ins": []}

    def dma(self, g, out_ap, in_ap, outs, ins):
        if self.rec is not None:
            self.rec.append(("dma", lambda: self._dma(g, out_ap, in_ap, outs, ins)))
            return
        self._dma(g, out_ap, in_ap, outs, ins)

    def _dma(self, g, out_ap, in_ap, outs, ins):
        q = g["q"]
        self._wait(q, self._deps(outs, ins))
        kw = {"max_dma_last_dim": 4096} if q == "pool" else {}
        self.E[q].dma_start(out=out_ap, in_=in_ap, **kw).then_inc(self.semh["d%d" % g["k"]], 16)
        self.ninst += 1
        self.dcum[g["k"]] += 16
        g["outs"] += outs
        g["ins"] += ins

    def dma_end(self, g, is_output=False):
        if self.rec is not None:
            self.rec.append(("dma", lambda: self._dma_end(g, is_output)))
            return
        self._dma_end(g, is_output)

    def _dma_end(self, g, is_output=False):
        ev = ("d%d" % g["k"], self.dcum[g["k"]], "dma")
        self.vc[(ev[0], ev[1])] = dict(self.waited[g["q"]])
        self._record(ev, g["outs"], g["ins"])
        if is_output:
            self.out_events.append(ev)

    def finish(self):
        for (sk, val, _) in self.out_events:
            if self.waited["sp"].get(sk, 0) < val:
                self.E["sp"].wait_ge(self.semh[sk], val)
                self.waited["sp"][sk] = val


class BG:
    def __init__(self):
        self.q = []
        self.i = 0
        self.on_empty = []

    def add(self, thunks):
        self.q += thunks

    def _levels_left(self):
        n = 0
        prev = None
        for k in range(self.i, len(self.q)):
            if self.q[k][0] != prev:
                n += 1
                prev = self.q[k][0]
        return n

    def step_for(self, tiles_left):
        if self.i >= len(self.q):
            return
        nlev = -(-self._levels_left() // max(1, int(tiles_left * BG_FRONTLOAD)))
        for _ in range(nlev):
            if self.i >= len(self.q):
                break
            eng = self.q[self.i][0]
            while self.i < len(self.q) and self.q[self.i][0] == eng:
                self.q[self.i][1]()
                self.i += 1

    def pending(self):
        return self.i < len(self.q)

    def flush(self):
        while self.i < len(self.q):
            self.q[self.i][1]()
            self.i += 1
        self.q = []
        self.i = 0
        hooks, self.on_empty = self.on_empty, []
        for h in hooks:
            h()


def build_program(nseq, layers):
    needed = None
    if PRUNE_INCS:
        nc0 = bass.Bass("TRN2", target_bir_lowering=False)
        with ExitStack() as es0:
            needed = _emit(nc0, es0, nseq, layers, None)
    nc = bass.Bass("TRN2", target_bir_lowering=False)
    es = ExitStack()
    with es:
        _emit(nc, es, nseq, layers, needed)
    return nc


def _emit(nc, es, nseq, layers, needed=None):
    def dram(name, shape, kind="ExternalInput"):
        return nc.dram_tensor(name, shape, F32, kind=kind).ap()

    x_d = dram("x", [nseq, S, D])
    mem_d = dram("mem", [nseq, NM, D])
    y_d = dram("y", [nseq, S, D], kind="ExternalOutput")
    w_in_d = dram("w_in", [L, D, INC])
    w_uq_d = dram("w_mla_uq", [L, 256, 384])
    w_ukv_d = dram("w_mla_ukv", [L, 128, 512])
    w_out_d = dram("w_out", [L, D, D])
    w_mq_d = dram("w_mem_q", [L, D, 256])
    w_mkv_d = dram("w_mem_kv", [L, D, 512])
    w_mo_d = dram("w_mem_o", [L, 256, D])
    w_f1_d = dram("w_ff1", [L, D, 4 * D])
    w_f2_d = dram("w_ff2", [L, 4 * D, D])
    gp_d = dram("gp", [128, GPN])
    gs_d = dram("gscale", [128, GPN])
    lamb_d = dram("lamb", [128, L * 256])
    bt_d = dram("bt", [128, 8, 2, 128])
    cst_d = dram("cst", [128, 768])
    rc_d = dram("ropec", [128, S])
    rs_d = dram("ropes", [128, S])

    T = Tracker(nc, es, needed)
    bg = BG()

    def sb(name, shape, dt):
        return es.enter_context(nc.sbuf_tensor(name, shape, dt))

    xT = sb("xT", [128, 8, S], F32)
    hT = sb("hT", [128, 8, S], BF16)
    mhT = sb("mhT", [128, 8, NM], F32)
    mnT = sb("mnT", [128, 8, NM], BF16)
    WB = [sb("WB%d" % i, [128, 8192], BF16) for i in range(2)]
    KT = sb("KT", [128, 2, S], BF16)
    VV = sb("VV", [128, 16, 128], BF16)
    QT = sb("QT", [128, 4, 512], BF16)
    yT = sb("yT", [128, 512], BF16)
    PT = [sb("PT%d" % i, [128, 512], BF16) for i in range(4)]
    FT = [sb("F%d" % i, [128, 512], F32) for i in range(8)]
    SM = [sb("SM%d" % i, [128, 128], F32) for i in range(2)]
    UU = [sb("UU%d" % i, [128, 1024], F32) for i in range(2)]
    cqnT = sb("cqnT", [128, 2, 512], BF16)
    ckvnT = sb("ckvnT", [128, 512], BF16)
    KTm = sb("KTm", [128, 2, NM], BF16)
    Vm = sb("Vm", [128, 2, 256], BF16)
    oTm = sb("oTm", [128, 2, 512], BF16)
    BT = sb("BT", [128, 2, 2, 128], F32)
    CST = sb("CST", [128, 768], F32)
    CSTB = sb("CSTB", [128, 640], BF16)
    SBX = sb("SBX", [128, 1024], F32)
    _sbxb = SBX[:].bitcast(BF16)
    SBH = [_sbxb[:, i * 512:(i + 1) * 512] for i in range(2)]
    SBL = [_sbxb[:, (2 + i) * 512:(3 + i) * 512] for i in range(2)]
    GP = sb("GP", [128, GPN], F32)
    GS = sb("GS", [128, GPN], F32)
    GD = sb("GD", [128, GPN], F32)
    LAM = sb("LAM", [128, 2 * L], F32)
    PS = [es.enter_context(nc.psum_tensor("ps%d" % i, [128, 512], F32)) for i in range(8)]

    def Fv(i, p0=0, p1=128, c0=0, c1=512):
        return V(FT[i][p0:p1, c0:c1], "f%d" % i)

    def PSv(i, p0=0, p1=128, c0=0, c1=512):
        return V(PS[i][p0:p1, c0:c1], "ps%d" % i)

    def PTv(i, p0=0, p1=128, c0=0, c1=512):
        return V(PT[i][p0:p1, c0:c1], "pt%d" % i)

    def xTv(kc, c):
        return V(xT[:, kc, c * 512:(c + 1) * 512], "xT.%d.%d" % (kc, c))

    def hTv(kc, c):
        return V(hT[:, kc, c * 512:(c + 1) * 512], "hT.%d.%d" % (kc, c))

    def hTtok(kc, t):
        return V(hT[:, kc, t * 128:(t + 1) * 128], "hT.%d.%d" % (kc, t // 4))

    IDENT = V(CST[:, 0:128], "cst")
    ONESF = V(CST[:, 128:256], "cst", "@onesf")
    TRI = V(CST[:, 256:384], "cst")
    BO64 = V(CST[:, 384:512], "cst", "@bo64")
    TRIMF = V(CST[:, 512:640], "cst")
    ONESB = V(CSTB[:, 0:128], "cstb")
    ONESB64 = V(CSTB[:, 0:64], "cstb")
    TRIMB = V(CSTB[:, 128:256], "cstb")
    TRIB = V(CSTB[:, 256:384], "cstb")
    TRIPB = V(CSTB[:, 384:512], "cstb")

    def ones_sub(p0, p1, m):
        return V(CST[p0:p1, 128:128 + m], "cst", "@ones:%d:%d:%d" % (p0, p1, m))

    def gd(col, p0=0, p1=128):
        return V(GD[p0:p1, col:col + 1], "gd")

    def _bf16_cols(ap):
        return ap.bitcast(BF16)[:, 0:ap.shape[-1]]

    def mm(out, lhsT, rhs, start, stop, inc=True):
        if BF16_STATS:
            tag = None
            for r in lhsT.res:
                if r.startswith("@"):
                    tag = r
            if tag is not None:
                if tag == "@onesf":
                    lhsT = V(CSTB[:, 0:128], "cstb")
                elif tag == "@bo64":
                    lhsT = V(CSTB[:, 512:640], "cstb")
                else:
                    p0_, p1_, m_ = [int(z) for z in tag.split(":")[1:]]
                    lhsT = V(CSTB[p0_:p1_, 0:m_], "cstb")
                rhs = V(_bf16_cols(rhs.ap), *rhs.res)
        T.op("pe", lambda e: e.matmul(out.ap, lhsT.ap, rhs.ap, start=start, stop=stop, skip_group_check=True),
             [out], [lhsT, rhs], inc=inc)

    def tr(out, in_, inc):
        T.op("pe", lambda e: e.transpose(out.ap, in_.ap, IDENT.ap), [out], [in_, IDENT], inc=True)

    def act(out, in_, func, bias=None, scale=1.0):
        ins = [in_]
        kw = {}
        if isinstance(bias, V):
            ins.append(bias)
            kw["bias"] = bias.ap
        elif bias is not None:
            kw["bias"] = float(bias)
        if scale != 1.0:
            kw["scale"] = float(scale)
        if BF16_STATS and func == AF.Square:
            out = V(_bf16_cols(out.ap), *out.res)
        T.op("act", lambda e: e.activation(out=out.ap, in_=in_.ap, func=func, **kw), [out], ins)

    def tt(eng, out, a, b, op):
        T.op(eng, lambda e: e.tensor_tensor(out.ap, a.ap, b.ap, op), [out], [a, b])

    def ts(eng, out, a, s1, op0):
        ins = [a]
        if isinstance(s1, V):
            ins.append(s1)
            s1v = s1.ap
        else:
            s1v = float(s1)
        T.op(eng, lambda e: e.tensor_scalar(out.ap, a.ap, s1v, None, op0), [out], ins)

    def stt(eng, out, a, s, b, op0, op1):
        ins = [a, b]
        if isinstance(s, V):
            ins.append(s)
            sv = s.ap
        else:
            sv = float(s)
        T.op(eng, lambda e: e.scalar_tensor_tensor(out.ap, a.ap, sv, b.ap, op0, op1), [out], ins)

    def recip(out, a):
        act(out, a, AF.Ln)
        act(out, out, AF.Exp, scale=-1.0)

    def memset(eng, out, val):
        T.op(eng, lambda e: e.memset(out.ap, val), [out], [])

    def rstd_from(dst, src_ps, neps):
        act(dst, src_ps, AF.Ln, bias=neps)
        act(dst, dst, AF.Exp, scale=-0.5)

    g = T.dma_begin("sp")
    T.dma(g, CST[:], cst_d, [V(None, "cst")], [])
    T.dma(g, GP[:], gp_d, [V(None, "gp")], [])
    T.dma(g, GS[:], gs_d, [V(None, "gs")], [])
    T.dma(g, FT[0][:], lamb_d[:, 0:512], [V(None, "f0")], [])
    T.dma(g, FT[1][:], lamb_d[:, 512:1024], [V(None, "f1")], [])
    T.dma_end(g)
    g = T.dma_begin("pool")
    T.dma(g, CSTB[:, 0:128], cst_d[:, 128:256], [V(None, "cstb")], [])
    T.dma(g, CSTB[:, 128:256], cst_d[:, 512:640], [V(None, "cstb")], [])
    T.dma(g, CSTB[:, 256:384], cst_d[:, 256:384], [V(None, "cstb")], [])
    T.dma(g, CSTB[:, 384:512], cst_d[:, 640:768], [V(None, "cstb")], [])
    T.dma(g, CSTB[:, 512:640], cst_d[:, 384:512], [V(None, "cstb")], [])
    T.dma_end(g)
    tt("dve", V(GD[:], "gd"), V(GP[:], "gp"), V(GS[:], "gs"), ALU.mult)
    for l in range(0 if not DEBUG_NOLAM else L, L):
        fsrc = FT[l // 2]
        fres = "f%d" % (l // 2)
        base = (l % 2) * 256
        for j in range(2):
            tt("dve", V(SM[0][:, 0:64], "sm0"), V(fsrc[:, base + j * 128:base + j * 128 + 64], fres),
               V(fsrc[:, base + j * 128 + 64:base + j * 128 + 128], fres), ALU.mult)
            T.op("dve", lambda e, j=j: e.reduce_sum(SM[1][:, j:j + 1], SM[0][:, 0:64], AX.X),
                 [V(None, "sm1")], [V(None, "sm0")])
        act(V(SM[1][:, 0:2], "sm1"), V(SM[1][:, 0:2], "sm1"), AF.Exp)
        lam_init = 0.8 - 0.6 * math.exp(-0.3 * l)
        tt("dve", V(SM[1][:, 2:3], "sm1"), V(SM[1][:, 1:2], "sm1"), V(SM[1][:, 0:1], "sm1"), ALU.subtract)
        ts("dve", V(LAM[:, 2 * l:2 * l + 1], "lam"), V(SM[1][:, 2:3], "sm1"), -lam_init, ALU.add)

    sets = []
    for s in range(nseq):
        for l in layers:
            for h in range(4):
                sets.append(("da", s, l, h))
            for p in range(2):
                sets.append(("sb", s, l, p))
            for p in range(2):
                sets.append(("ml", s, l, p))
            sets.append(("mem", s, l, 0))
            for gidx in range(8):
                sets.append(("ff", s, l, gidx))

    def wview(half, off, n, k=None):
        ap = WB[half][:, off:off + n]
        if k is not None:
            ap = ap.rearrange("p (k n) -> p k n", k=k)
        return ap

    def kc_view(d_ap):
        return d_ap.rearrange("(k p) n -> p k n", p=128)

    def load_set(i):
        kind, s, l, u = sets[i]
        if DEBUG_NOSETS or (DEBUG_SETKINDS is not None and kind not in DEBUG_SETKINDS):
            return
        half = i % 2
        res = [V(None, "wb%d" % half)]
        g = T.dma_begin("pool")

        def ld(off, n, k, src):
            T.dma(g, wview(half, off, n, k), src, res, [])

        win = kc_view(w_in_d[l])
        if kind in ("da", "sb"):
            if kind == "da":
                qc, kcol, vc, orow = u * 128, 512 + u * 128, 1024 + u * 128, u * 128
            else:
                qc, kcol, vc, orow = 1536 + u * 128, 1792 + u * 128, 2048 + u * 128, 512 + u * 128
            ld(0, 1024, 8, win[:, :, qc:qc + 128])
            ld(1024, 1024, 8, win[:, :, kcol:kcol + 128])
            ld(2048, 1024, 8, win[:, :, vc:vc + 128])
            ld(3072, 1024, None, w_out_d[l, orow:orow + 128, :])
        elif kind == "ml":
            ld(0, 2048, 8, win[:, :, 2304:2560])
            ld(2048, 1024, 8, win[:, :, 2560:2688])
            ld(3072, 256, 8, win[:, :, 2688:2720])
            T.dma(g, wview(half, 3328, 256, 8)[:, :, 16:32], win[:, :, 2688:2704], res, [])
            T.dma(g, wview(half, 3328, 256, 8)[:, :, 0:16], win[:, :, 2704:2720], res, [])
            uq = w_uq_d[l].rearrange("(k p) n -> p k n", p=128)
            ld(3584, 384, 2, uq[:, :, u * 192:(u + 1) * 192])
            wp = wview(half, 3968, 128, 2)
            for hh in range(2):
                c96 = (2 * u + hh) * 96
                T.dma(g, wp[:, :, hh * 32:hh * 32 + 16], uq[:, :, c96 + 80:c96 + 96], res, [])
                T.dma(g, wp[:, :, hh * 32 + 16:hh * 32 + 32], uq[:, :, c96 + 64:c96 + 80], res, [])
            ukv = w_ukv_d[l]
            for hh in range(2):
                h = 2 * u + hh
                T.dma(g, wview(half, 4096 + hh * 64, 64), ukv[:, h * 128:h * 128 + 64], res, [])
                T.dma(g, wview(half, 4224 + hh * 64, 64), ukv[:, h * 128 + 64:h * 128 + 128], res, [])
            orow = 768 + u * 128
            ld(4352, 1024, None, w_out_d[l, orow:orow + 128, :])
        elif kind == "mem":
            ld(0, 2048, 8, kc_view(w_mq_d[l]))
            ld(2048, 4096, 8, kc_view(w_mkv_d[l]))
            ld(6144, 2048, 2, w_mo_d[l].rearrange("(k p) n -> p k n", p=128))
        else:
            ld(0, 4096, 8, kc_view(w_f1_d[l])[:, :, u * 512:(u + 1) * 512])
            ld(4096, 4096, 4, w_f2_d[l, u * 512:(u + 1) * 512, :].rearrange("(k p) n -> p k n", p=128))
        T.dma_end(g)

    def xnorm_chunk(gcol, c, temps=None):
        sqa, sqb, rs, pb = temps if temps is not None else ((0, 1, 2, 7), (3, 4, 5, 6))[c % 2]
        for kc in range(8):
            sq = Fv(sqa if kc % 2 == 0 else sqb)
            act(sq, xTv(kc, c), AF.Square)
            mm(PSv(pb), ONESF, sq, kc == 0, kc == 7)
        rstd_from(Fv(rs), PSv(pb), D * EPS)
        for kc in range(8):
            stt("dve", hTv(kc, c), xTv(kc, c), gd(gcol + kc), Fv(rs), ALU.mult, ALU.mult)

    def xnorm(gcol):
        for c in range(4):
            xnorm_chunk(gcol, c)

    def wout_add(wo, rhs_list, c):
        n = len(rhs_list)
        for cb in range(8):
            bank = PSv(6 + cb % 2)
            for k in range(n):
                mm(bank, V(wo[k][:, cb * 128:(cb + 1) * 128], wo_res), rhs_list[k], k == 0, k == n - 1)
            tt("dve", xTv(cb, c), xTv(cb, c), bank, ALU.add)

    wo_res = "wb0"

    def proj8(bank, wv, wres, c, m0=0, m1=128):
        for kc in range(8):
            mm(bank, V(wv[:, kc, :], wres), hTv(kc, c), kc == 0, kc == 7, inc=(kc == 7))

    def groupnorm_to(dst, src_ps, ones_v, neps, gcol, sq=0, rs=1, p0=0, p1=128, n=512, dsts=None):
        act(Fv(sq, p0, p1, 0, n), src_ps, AF.Square)
        mm(PSv(7, p0, p1, 0, n), ones_v, Fv(sq, p0, p1, 0, n), True, True)
        rstd_from(Fv(rs, p0, p1, 0, n), PSv(7, p0, p1, 0, n), neps)
        if dsts is None:
            stt("dve", dst, src_ps, gd(gcol, p0, p1), Fv(rs, p0, p1, 0, n), ALU.mult, ALU.mult)
        else:
            for (dv, a, b) in dsts:
                stt("dve", dv, V(src_ps.ap[a:b, :], *src_ps.res), gd(gcol, a, b), Fv(rs, a, b, 0, n), ALU.mult, ALU.mult)

    def pe_dummy():
        T.op("pe", lambda e: e.matmul(PS[5][:, 0:512], CSTB[:, 0:128], CSTB[:, 0:512], start=True, stop=True,
                                      skip_group_check=True), [], [], inc=False)

    def zero_kt_halves():
        allk0 = ["kt0.%d" % c for c in range(4)]
        allk1 = ["kt1.%d" % c for c in range(4)]
        memset("pool", V(KT[64:128, 0, :], *allk0), 0.0)
        memset("pool", V(KT[0:64, 1, :], *allk1), 0.0)

    def unit_da(i, l, h, defer_last=False):
        half = i % 2
        wres = "wb%d" % half
        nonlocal wo_res
        wo_res = wres
        gb = l * GPL
        wq = wview(half, 0, 1024, 8)
        wk = wview(half, 1024, 1024, 8)
        wv = wview(half, 2048, 1024, 8)
        wo = wview(half, 3072, 1024)
        g = T.dma_begin("sp")
        T.dma(g, BT[:], bt_d[:, 2 * h:2 * h + 2, :, :], [V(None, "bt")], [])
        T.dma_end(g)
        for m_ in range(2):
            ts("dve", V(BT[:, m_, :, :], "bt"), V(BT[:, m_, :, :], "bt"), gd(GPL * L + 2 * h + m_), ALU.subtract)
        memset("pool", V(BT[64:128, :, 0, 0:64], "bt"), NEG)
        zero_kt_halves()

        def prep(c):
            cs = slice(c * 512, (c + 1) * 512)
            sl = c % 2
            proj8(PSv(6), wq, wres, c)
            groupnorm_to(V(QT[:, sl, :], "qt%d" % sl), PSv(6), BO64, 64 * EPS, gb + 32)
            proj8(PSv(6), wk, wres, c)
            groupnorm_to(None, PSv(6), BO64, 64 * EPS, gb + 33,
                         dsts=[(V(KT[0:64, 0, cs], "kt0.%d" % c), 0, 64), (V(KT[64:128, 1, cs], "kt1.%d" % c), 64, 128)])
            for t4 in range(4):
                t = 4 * c + t4
                bank = PSv(6 + t4 % 2, 0, 128, 0, 128)
                for kc in range(8):
                    mm(bank, hTtok(kc, t), V(wv[:, kc, :], wres), kc == 0, kc == 7, inc=(kc == 7))
                act(V(VV[:, t, :], "vv.%d" % t), bank, AF.Copy)

        def prep0():
            c = 0
            cs = slice(0, 512)
            proj8(PSv(6), wq, wres, c)
            proj8(PSv(5), wk, wres, c)
            for t4 in range(4):
                bank = PSv(t4, 0, 128, 0, 128)
                for kc in range(8):
                    mm(bank, hTtok(kc, t4), V(wv[:, kc, :], wres), kc == 0, kc == 7, inc=(kc == 7))
            act(Fv(0), PSv(6), AF.Square)
            act(Fv(6), PSv(5), AF.Square)
            mm(PSv(7), BO64, Fv(0), True, True)
            mm(PSv(4), BO64, Fv(6), True, True)
            for t4 in range(4):
                act(V(VV[:, t4, :], "vv.%d" % t4), PSv(t4, 0, 128, 0, 128), AF.Copy)
            rstd_from(Fv(1), PSv(7), 64 * EPS)
            rstd_from(Fv(7), PSv(4), 64 * EPS)
            stt("dve", V(QT[:, 0, :], "qt0"), PSv(6), gd(gb + 32), Fv(1), ALU.mult, ALU.mult)
            stt("dve", V(KT[0:64, 0, cs], "kt0.0"), PSv(5, 0, 64), gd(gb + 33, 0, 64), Fv(7, 0, 64), ALU.mult, ALU.mult)
            stt("dve", V(KT[64:128, 1, cs], "kt1.0"), PSv(5, 64, 128), gd(gb + 33, 64, 128), Fv(7, 64, 128), ALU.mult, ALU.mult)

        def tiles(c):
            sl = c % 2
            tl = [(m, kb) for m in range(2) for kb in range(4 * c + 4)]
            nkb = 4 * c + 4

            zb = (0, 1, 5) if DUMMY_MM == 0 else (0, 1)
            LA = len(zb) - 1

            def stA(j):
                m, kb = tl[j]
                c0 = max(0, kb - 4 * c) * 128
                mm(PSv(zb[j % len(zb)], 0, 128, c0, 512),
                   V(KT[:, m, kb * 128:(kb + 1) * 128], "kt%d.%d" % (m, kb // 4)),
                   V(QT[:, sl, c0:512], "qt%d" % sl), True, True)

            def stB(j):
                m, kb = tl[j]
                qs0 = max(0, kb - 4 * c)
                cq0 = min(4, max(0, kb - 4 * c + 2))
                bank = zb[j % len(zb)]
                if cq0 > qs0:
                    j0 = 0 if (kb - 4 * c - qs0) == 0 else 1
                    nb = cq0 - qs0
                    T.op("dve", lambda e: e.tensor_tensor(
                        PS[bank][:, qs0 * 128:cq0 * 128].rearrange("p (a b) -> p a b", a=nb),
                        PS[bank][:, qs0 * 128:cq0 * 128].rearrange("p (a b) -> p a b", a=nb),
                        BT[:, m, j0:j0 + nb, :], ALU.add),
                        [V(None, "ps%d" % bank)], [V(None, "ps%d" % bank), V(None, "bt")])
                act(PTv(j % 4, 0, 128, qs0 * 128, 512), PSv(bank, 0, 128, qs0 * 128, 512), AF.Exp)

            def stC(j):
                m, kb = tl[j]
                c0 = max(0, kb - 4 * c) * 128
                mm(PSv(2 + m, 0, 128, c0, 512), V(VV[:, kb, :], "vv.%d" % kb), PTv(j % 4, 0, 128, c0, 512),
                   kb == 0, kb == nkb - 1)
                if DEN_ON_PE:
                    mm(PSv(4, 0, 128, c0, 512), ONESB, PTv(j % 4, 0, 128, c0, 512), kb == 0, kb == nkb - 1)
                    if kb == nkb - 1:
                        recip(Fv(4 + m), PSv(4))
                elif kb == 0:
                    T.op("dve", lambda e, d_=Fv(4 + m), p_=PTv(j % 4): e.tensor_copy(d_.ap, p_.ap), [Fv(4 + m)], [PTv(j % 4)])
                else:
                    tt("dve", Fv(4 + m, 0, 128, c0, 512), Fv(4 + m, 0, 128, c0, 512), PTv(j % 4, 0, 128, c0, 512), ALU.add)
                if kb == nkb - 1 and not DEN_ON_PE:
                    mm(PSv(4), ONESF, Fv(4 + m), True, True)
                    recip(Fv(4 + m), PSv(4))

            n = len(tl)
            for j0 in range(min(LA, n)):
                stA(j0)
            for j in range(n):
                if j + LA < n:
                    stA(j + LA)
                stB(j)
                bg.step_for(n - j)
                stC(j)
                for _ in range(DUMMY_MM):
                    pe_dummy()
            bg.flush()

        def post_a(c):
            tt("dve", Fv(2), PSv(2), Fv(4), ALU.mult)
            tt("dve", Fv(3), PSv(3), Fv(5), ALU.mult)

        def post_b(c):
            stt("dve", Fv(2), Fv(3), V(LAM[:, 2 * l:2 * l + 1], "lam"), Fv(2), ALU.mult, ALU.add)
            act(Fv(0), Fv(2), AF.Square)
            mm(PSv(7), ONESF, Fv(0), True, True)
            rstd_from(Fv(1), PSv(7), 128 * EPS)
            stt("dve", V(yT[:], "yt"), Fv(2), gd(gb + 34), Fv(1), ALU.mult, ALU.mult)
            wout_add([wo], [V(yT[:], "yt")], c)

        run_unit(prep, tiles, post_a, post_b, prep0=prep0, defer_last=defer_last)

    def run_unit(prep, tiles, post_a, post_b_, prep_thunks=None, tail=None, prep0=None, defer_last=False):
        def post_b(c):
            post_b_(c)
            if tail is not None:
                tail(c)

        if prep_thunks is None:
            def prep_thunks(c):
                return T.record(lambda: prep(c))
        if prep0 is not None:
            prep0()
        else:
            for th in prep_thunks(0):
                th[1]()
        for c in range(4):
            if c + 1 < 4:
                bg.add(prep_thunks(c + 1))
            tiles(c)
            post_a(c)
            if PIPELINE and (c < 3 or defer_last):
                bg.add(T.record(lambda: post_b(c)))
            else:
                post_b(c)

    def unit_sb(i, l, p, defer_last=False):
        half = i % 2
        wres = "wb%d" % half
        nonlocal wo_res
        wo_res = wres
        gb = l * GPL
        wq = wview(half, 0, 1024, 8)
        wk = wview(half, 1024, 1024, 8)
        wv = wview(half, 2048, 1024, 8)
        wo = wview(half, 3072, 1024)

        def prep(c):
            cs = slice(c * 512, (c + 1) * 512)
            sl = c % 2
            proj8(PSv(6), wq, wres, c)
            act(V(QT[:, sl, :], "qt%d" % sl), PSv(6), AF.Copy, scale=0.125)
            proj8(PSv(7), wk, wres, c)
            act(V(KT[:, 0, cs], "kt0.%d" % c), PSv(7), AF.Copy)
            for t4 in range(4):
                t = 4 * c + t4
                bank = PSv(6 + t4 % 2, 0, 128, 0, 128)
                for kc in range(8):
                    mm(bank, hTtok(kc, t), V(wv[:, kc, :], wres), kc == 0, kc == 7, inc=(kc == 7))
                act(V(VV[:, t, :], "vv.%d" % t), bank, AF.Copy)

        def prep0():
            c = 0
            proj8(PSv(6), wq, wres, c)
            proj8(PSv(7), wk, wres, c)
            for t4 in range(4):
                bank = PSv(t4, 0, 128, 0, 128)
                for kc in range(8):
                    mm(bank, hTtok(kc, t4), V(wv[:, kc, :], wres), kc == 0, kc == 7, inc=(kc == 7))
            act(V(QT[:, 0, :], "qt0"), PSv(6), AF.Copy, scale=0.125)
            act(V(KT[:, 0, 0:512], "kt0.0"), PSv(7), AF.Copy)
            for t4 in range(4):
                act(V(VV[:, t4, :], "vv.%d" % t4), PSv(t4, 0, 128, 0, 128), AF.Copy)

        def tiles(c):
            sl = c % 2
            nkb = 4 * c + 4
            kbs = list(range(nkb - 1, -1, -1))
            HH = (0, 1)
            tsets = [(0, 1), (2, 3)]
            zb = [0, 1]
            accb = [3, 2]

            def c0_of(kb):
                return max(0, kb - 4 * c) * 128

            def hl(hh, c0):
                return (V(SBH[hh][:, c0:512], "sbh%d" % hh), V(SBL[hh][:, c0:512], "sbl%d" % hh))

            def stA(j, hh):
                kb = kbs[j]
                c0 = c0_of(kb)
                o = hh * 64
                mm(PSv(zb[hh], 0, 128, c0, 512),
                   V(KT[o:o + 64, 0, kb * 128:(kb + 1) * 128], "kt0.%d" % (kb // 4)),
                   V(QT[o:o + 64, sl, c0:512], "qt%d" % sl), True, True)

            def stB1(j, hh):
                kb = kbs[j]
                c0 = c0_of(kb)
                fe, fs = tsets[hh]
                hi, lo = hl(hh, c0)
                act(Fv(fe, 0, 128, c0, 512), PSv(zb[hh], 0, 128, c0, 512), AF.Exp)
                act(Fv(fs, 0, 128, c0, 512), Fv(fe, 0, 128, c0, 512), AF.Ln, bias=1.0)
                if kb >= 4 * c:
                    tt("pool", Fv(fs, 0, 128, c0, c0 + 128), Fv(fs, 0, 128, c0, c0 + 128), TRIMF, ALU.mult)
                T.op("dve", lambda e, h_=hi, s_=Fv(fs, 0, 128, c0, 512): e.tensor_copy(h_.ap, s_.ap), [hi], [Fv(fs)])
                tt("pool", lo, Fv(fs, 0, 128, c0, 512), hi, ALU.subtract)
                tt("dve", Fv(fe, 0, 128, c0, 512), PSv(zb[hh], 0, 128, c0, 512), Fv(fs, 0, 128, c0, 512), ALU.subtract)

            def stC1(j, hh):
                kb = kbs[j]
                c0 = c0_of(kb)
                hi, lo = hl(hh, c0)
                mm(PSv(accb[hh], 0, 128, c0, 512), TRIB, hi, j == 0, False)
                mm(PSv(accb[hh], 0, 128, c0, 512), TRIB, lo, False, False)

            def stB2(j, hh):
                kb = kbs[j]
                c0 = c0_of(kb)
                fe, fs = tsets[hh]
                pt = 2 * hh + j % 2
                tt("dve", Fv(fe, 0, 128, c0, 512), Fv(fe, 0, 128, c0, 512), PSv(accb[hh], 0, 128, c0, 512), ALU.subtract)
                act(PTv(pt, 0, 128, c0, 512), Fv(fe, 0, 128, c0, 512), AF.Exp)
                if kb >= 4 * c:
                    tt("pool", PTv(pt, 0, 128, c0, c0 + 128), PTv(pt, 0, 128, c0, c0 + 128), TRIMB, ALU.mult)

            def stC2(j, hh):
                kb = kbs[j]
                c0 = c0_of(kb)
                o = hh * 64
                hi, lo = hl(hh, c0)
                pt = 2 * hh + j % 2
                mm(PSv(accb[hh], 0, 128, c0, 512), TRIPB, hi, False, False)
                mm(PSv(accb[hh], 0, 128, c0, 512), TRIPB, lo, False, j == nkb - 1)
                mm(PSv(4, o, o + 64, c0, 512), V(VV[:, kb, o:o + 64], "vv.%d" % kb),
                   PTv(pt, 0, 128, c0, 512), j == 0, j == nkb - 1)

            for hh in HH:
                stA(0, hh)
            for hh in HH:
                stB1(0, hh)
            for j in range(nkb):
                for hh in HH:
                    stC1(j, hh)
                bg.step_for(2 * (nkb - j))
                for hh in HH:
                    stB2(j, hh)
                if j + 1 < nkb:
                    for hh in HH:
                        stA(j + 1, hh)
                bg.step_for(2 * (nkb - j) - 1)
                for hh in HH:
                    stC2(j, hh)
                if j + 1 < nkb:
                    for hh in HH:
                        stB1(j + 1, hh)
            bg.flush()

        def post_a(c):
            act(Fv(5), PSv(4), AF.Copy)

        def post_b(c):
            act(Fv(6), Fv(5), AF.Square)
            mm(PSv(7), BO64, Fv(6), True, True)
            rstd_from(Fv(6), PSv(7), 64 * EPS)
            stt("dve", V(yT[:], "yt"), Fv(5), gd(gb + 35), Fv(6), ALU.mult, ALU.mult)
            wout_add([wo], [V(yT[:], "yt")], c)

        run_unit(prep, tiles, post_a, post_b, prep0=prep0, defer_last=defer_last)

    def unit_ml(i, l, p):
        half = i % 2
        wres = "wb%d" % half
        nonlocal wo_res
        wo_res = wres
        gb = l * GPL
        wcq = wview(half, 0, 2048, 8)
        wckv = wview(half, 2048, 1024, 8)
        wkr = wview(half, 3072, 256, 8)
        wkrp = wview(half, 3328, 256, 8)
        wuq = wview(half, 3584, 384, 2)
        wuqp = wview(half, 3968, 128, 2)
        wuk = wview(half, 4096, 128)
        wuv = wview(half, 4224, 128)
        wo = wview(half, 4352, 1024)

        def Gv(i, p0=0, p1=128, c0=0, c1=512):
            buf = UU[i // 2]
            off = (i % 2) * 512
            return V(buf[p0:p1, off + c0:off + c1], "uu%d%s" % (i // 2, "ab"[i % 2]))

        def prep_q(c):
            for j in range(2):
                for kc in range(8):
                    mm(PSv(5 + j), V(wcq[:, kc, j * 128:(j + 1) * 128], wres), hTv(kc, c), kc == 0, kc == 7, inc=(kc == 7))
            act(Fv(0), PSv(5), AF.Square)
            act(Fv(1), PSv(6), AF.Square)
            mm(PSv(7), ONESF, Fv(0), True, False)
            mm(PSv(7), ONESF, Fv(1), False, True)
            rstd_from(Fv(2), PSv(7), 256 * EPS)
            for j in range(2):
                stt("dve", V(cqnT[:, j, :], "cqn"), PSv(5 + j), gd(gb + 36 + j), Fv(2), ALU.mult, ALU.mult)
            for hh in range(2):
                sl = (c % 2) * 2 + hh
                qres = "qt%d" % sl
                for j in range(2):
                    mm(PSv(5, 0, 96), V(wuq[:, j, hh * 96:(hh + 1) * 96], wres), V(cqnT[:, j, :], "cqn"), j == 0, j == 1)
                for j in range(2):
                    mm(PSv(6, 64, 96), V(wuqp[:, j, hh * 32:(hh + 1) * 32], wres), V(cqnT[:, j, :], "cqn"), j == 0, j == 1)
                act(Fv(0, 0, 96), PSv(5, 0, 96), AF.Square)
                mm(PSv(7, 0, 96), ones_sub(0, 96, 96), Fv(0, 0, 96), True, True)
                rstd_from(Fv(1, 0, 96), PSv(7, 0, 96), 96 * EPS)
                stt("dve", V(QT[0:64, sl, :], qres), PSv(5, 0, 64), gd(gb + 39, 0, 64), Fv(1, 0, 64), ALU.mult, ALU.mult)
                stt("dve", Fv(0, 64, 96), PSv(5, 64, 96), gd(gb + 39, 64, 96), Fv(1, 64, 96), ALU.mult, ALU.mult)
                stt("dve", Fv(2, 64, 96), PSv(6, 64, 96), gd(gb + 40, 64, 96), Fv(1, 64, 96), ALU.mult, ALU.mult)
                tt("dve", Fv(0, 64, 96), Fv(0, 64, 96), Fv(3, 64, 96), ALU.mult)
                tt("dve", Fv(2, 64, 96), Fv(2, 64, 96), Fv(4, 64, 96), ALU.mult)
                tt("dve", V(QT[64:96, sl, :], qres), Fv(0, 64, 96), Fv(2, 64, 96), ALU.add)

        def prep_k(c):
            cs = slice(c * 512, (c + 1) * 512)
            g = T.dma_begin("sp")
            T.dma(g, FT[3][64:96, :], rc_d[64:96, cs], [V(None, "f3")], [])
            T.dma(g, FT[4][64:96, :], rs_d[64:96, cs], [V(None, "f4")], [])
            T.dma_end(g)
            proj8(PSv(4), wckv, wres, c)
            act(Gv(0), PSv(4), AF.Square)
            mm(PSv(7), ONESF, Gv(0), True, True)
            rstd_from(Gv(1), PSv(7), 128 * EPS)
            stt("dve", V(ckvnT[:], "ckvn"), PSv(4), gd(gb + 38), Gv(1), ALU.mult, ALU.mult)
            for kc in range(8):
                mm(PSv(4, 64, 96), V(wkr[:, kc, :], wres), hTv(kc, c), kc == 0, kc == 7, inc=(kc == 7))
            act(Fv(5, 64, 96), PSv(4, 64, 96), AF.Copy)
            act(Fv(7, 64, 96), PSv(4, 64, 96), AF.Square)
            for kc in range(8):
                mm(PSv(4, 64, 96), V(wkrp[:, kc, :], wres), hTv(kc, c), kc == 0, kc == 7, inc=(kc == 7))
            act(Fv(6, 64, 96), PSv(4, 64, 96), AF.Copy)
            for t4 in range(4):
                t = 4 * c + t4
                bank = PSv(4, 0, 128, 0, 128)
                mm(bank, V(ckvnT[:, t4 * 128:(t4 + 1) * 128], "ckvn"), V(wuv, wres), True, True)
                act(V(VV[:, t, :], "vv.%d" % t), bank, AF.Copy)
            for hh in range(2):
                mm(PSv(4, 0, 64), V(wuk[:, hh * 64:(hh + 1) * 64], wres), V(ckvnT[:], "ckvn"), True, True)
                act(Gv(0, 0, 64), PSv(4, 0, 64), AF.Square)
                if BF16_STATS:
                    T.op("pool", lambda e: e.tensor_copy(_bf16_cols(UU[0][64:96, 0:512]), _bf16_cols(FT[7][64:96, :])),
                         [Gv(0)], [Fv(7)])
                else:
                    T.op("pool", lambda e: e.tensor_copy(UU[0][64:96, 0:512], FT[7][64:96, :]), [Gv(0)], [Fv(7)])
                mm(PSv(7, 0, 96), ones_sub(0, 96, 96), Gv(0, 0, 96), True, True)
                rstd_from(Gv(1, 0, 96), PSv(7, 0, 96), 96 * EPS)
                ktr = "kt%d.%d" % (hh, c)
                stt("dve", V(KT[0:64, hh, cs], ktr), PSv(4, 0, 64), gd(gb + 41, 0, 64), Gv(1, 0, 64), ALU.mult, ALU.mult)
                stt("dve", Gv(0, 64, 96), Fv(5, 64, 96), gd(gb + 41, 64, 96), Gv(1, 64, 96), ALU.mult, ALU.mult)
                stt("dve", Gv(2, 64, 96), Fv(6, 64, 96), gd(gb + 42, 64, 96), Gv(1, 64, 96), ALU.mult, ALU.mult)
                tt("dve", Gv(0, 64, 96), Gv(0, 64, 96), Fv(3, 64, 96), ALU.mult)
                tt("dve", Gv(2, 64, 96), Gv(2, 64, 96), Fv(4, 64, 96), ALU.mult)
                tt("dve", V(KT[64:96, hh, cs], ktr), Gv(0, 64, 96), Gv(2, 64, 96), ALU.add)

        def prep_thunks(c):
            qa = T.record(lambda: prep_q(c))
            ka = T.record(lambda: prep_k(c))
            out = []
            i = j = 0
            while i < len(qa) or j < len(ka):
                if j >= len(ka) or (i < len(qa) and i * len(ka) <= j * len(qa)):
                    out.append(qa[i])
                    i += 1
                else:
                    out.append(ka[j])
                    j += 1
            return out

        def tiles(c):
            nkb = 4 * c + 4
            for hh in range(2):
                o = hh * 64
                sl = (c % 2) * 2 + hh
                qres = "qt%d" % sl

                def c0_of(kb):
                    return max(0, kb - 4 * c) * 128

                def stA(kb):
                    c0 = c0_of(kb)
                    mm(PSv(kb % 2, 0, 128, c0, 512),
                       V(KT[0:96, hh, kb * 128:(kb + 1) * 128], "kt%d.%d" % (hh, kb // 4)),
                       V(QT[0:96, sl, c0:512], qres), True, True)

                def stB(kb):
                    c0 = c0_of(kb)
                    act(PTv(kb % 4, 0, 128, c0, 512), PSv(kb % 2, 0, 128, c0, 512), AF.Exp)
                    if kb >= 4 * c:
                        memset("pool", PTv(kb % 4, 64, 128, c0, c0 + 64), 0.0)

                def stC(kb):
                    c0 = c0_of(kb)
                    mm(PSv(2, o, o + 64, c0, 512), V(VV[:, kb, o:o + 64], "vv.%d" % kb),
                       PTv(kb % 4, 0, 128, c0, 512), kb == 0, kb == nkb - 1)
                    if DEN_ON_PE:
                        mm(PSv(3, o, o + 64, c0, 512), ONESB64, PTv(kb % 4, 0, 128, c0, 512), kb == 0, kb == nkb - 1)
                    elif kb == 0:
                        T.op("dve", lambda e, p_=PTv(kb % 4): e.tensor_copy(SBX[:, 0:512], p_.ap),
                             [V(None, "sbh0", "sbh1")], [PTv(kb % 4)])
                    else:
                        tt("dve", V(SBX[:, c0:512], "sbh0", "sbh1"), V(SBX[:, c0:512], "sbh0", "sbh1"),
                           PTv(kb % 4, 0, 128, c0, 512), ALU.add)
                    if kb == nkb - 1 and not DEN_ON_PE:
                        mm(PSv(3, o, o + 64), V(CST[:, 128:192], "cst"), V(SBX[:, 0:512], "sbh0", "sbh1"), True, True)

                stA(0)
                for kb in range(nkb):
                    if kb + 1 < nkb:
                        stA(kb + 1)
                    stB(kb)
                    bg.step_for((2 - hh) * nkb - kb)
                    stC(kb)
            bg.flush()

        def post_a(c):
            recip(Fv(0), PSv(3))
            tt("dve", Fv(0), PSv(2), Fv(0), ALU.mult)

        def post_b(c):
            act(Fv(1), Fv(0), AF.Square)
            mm(PSv(7), BO64, Fv(1), True, True)
            rstd_from(Fv(2), PSv(7), 64 * EPS)
            stt("dve", V(yT[:], "yt"), Fv(0), gd(gb + 43), Fv(2), ALU.mult, ALU.mult)
            wout_add([wo], [V(yT[:], "yt")], c)

        tail = None
        if p == 1 and XNORM_PIPE:
            def tail(c):
                xnorm_chunk(gb + 8, c, temps=(0, 1, 2, 7))
        run_unit(None, tiles, post_a, post_b, prep_thunks=prep_thunks, tail=tail)

    def unit_mem(i, l):
        half = i % 2
        wres = "wb%d" % half
        nonlocal wo_res
        wo_res = wres
        gb = l * GPL
        wq = wview(half, 0, 2048, 8)
        wkv = wview(half, 2048, 4096, 8)
        wo = wview(half, 6144, 2048, 2)
        if not XNORM_PIPE:
            xnorm(gb + 8)
        for kc in range(8):
            ts("dve", V(mnT[:, kc, :], "mn"), V(mhT[:, kc, :], "mh"), gd(gb + 24 + kc), ALU.mult)
        for p in range(2):
            for kc in range(8):
                mm(PSv(6, 0, 128, 0, 256), V(wkv[:, kc, p * 128:(p + 1) * 128], wres), V(mnT[:, kc, :], "mn"), kc == 0, kc == 7)
            groupnorm_to(V(KTm[:, p, :], "ktm"), PSv(6, 0, 128, 0, 256), BO64, 64 * EPS, gb + 45, n=256)
        for mb in range(2):
            for kc in range(8):
                mm(PSv(6, 0, 128, 0, 256), V(mnT[:, kc, mb * 128:(mb + 1) * 128], "mn"), V(wkv[:, kc, 256:512], wres), kc == 0, kc == 7)
            act(V(Vm[:, mb, :], "vm"), PSv(6, 0, 128, 0, 256), AF.Copy)
        for c in range(4):
            for p in range(2):
                for kc in range(8):
                    mm(PSv(6), V(wq[:, kc, p * 128:(p + 1) * 128], wres), hTv(kc, c), kc == 0, kc == 7, inc=(kc == 7))
                groupnorm_to(V(QT[:, p, :], "qt%d" % p), PSv(6), BO64, 64 * EPS, gb + 44)
            for p in range(2):
                tl = [(hh, mb) for hh in range(2) for mb in range(2)]

                def stA(j):
                    hh, mb = tl[j]
                    o = hh * 64
                    mm(PSv(j % 2), V(KTm[o:o + 64, p, mb * 128:(mb + 1) * 128], "ktm"),
                       V(QT[o:o + 64, p, :], "qt%d" % p), True, True)

                stA(0)
                for j in range(4):
                    hh, mb = tl[j]
                    o = hh * 64
                    h = 2 * p + hh
                    if j + 1 < 4:
                        stA(j + 1)
                    act(PTv(j % 4), PSv(j % 2), AF.Exp)
                    mm(PSv(2, o, o + 64), V(Vm[:, mb, h * 64:(h + 1) * 64], "vm"), PTv(j % 4), mb == 0, mb == 1)
                    mm(PSv(3, o, o + 64), ONESB64, PTv(j % 4), mb == 0, mb == 1)
                recip(Fv(0), PSv(3))
                tt("dve", V(oTm[:, p, :], "otm%d" % p), PSv(2), Fv(0), ALU.mult)
            wout_add([wo[:, 0, :], wo[:, 1, :]], [V(oTm[:, 0, :], "otm0"), V(oTm[:, 1, :], "otm1")], c)
            if XNORM_PIPE:
                xnorm_chunk(gb + 16, c, temps=(2, 3, 4, 5))

    def unit_ff(i, l, gidx, next_gb=None):
        half = i % 2
        wres = "wb%d" % half
        nonlocal wo_res
        wo_res = wres
        gb = l * GPL
        if gidx == 0 and not XNORM_PIPE:
            xnorm(gb + 16)
        W1 = wview(half, 0, 4096, 8)
        W2 = wview(half, 4096, 4096, 4)

        def ubuf(c):
            ui = (gidx * 4 + c) % 2
            return UU[ui][:].bitcast(BF16).rearrange("p (k n) -> p k n", k=4), ("uu%da" % ui, "uu%db" % ui)

        def ff1(c):
            u, ures = ubuf(c)
            for fc in range(4):
                bank = PSv(fc)
                for kc in range(8):
                    mm(bank, V(W1[:, kc, fc * 128:(fc + 1) * 128], wres), hTv(kc, c), kc == 0, kc == 7, inc=(kc == 7))
                act(Fv(fc), bank, AF.Relu)
                tt("pool", V(u[:, fc, :], *ures), Fv(fc), Fv(fc), ALU.mult)

        def ff2(c):
            u, ures = ubuf(c)
            for cb in range(8):
                bank = PSv(4 + cb % 3)
                for fc in range(4):
                    mm(bank, V(W2[:, fc, cb * 128:(cb + 1) * 128], wres), V(u[:, fc, :], *ures), fc == 0, fc == 3, inc=(fc == 3))
                tt("dve", xTv(cb, c), xTv(cb, c), bank, ALU.add)

        ff1(0)
        for c in range(4):
            if c + 1 < 4:
                ff1(c + 1)
            ff2(c)
            if gidx == 7 and XNORM_PIPE and next_gb is not None:
                xnorm_chunk(next_gb, c, temps=(4, 5, 6, 7))

    si = 0
    load_set(0)
    for s in range(nseq):
        for t in range(16):
            st = UU[t % 2]
            sres = ("uu%da" % (t % 2), "uu%db" % (t % 2))
            g = T.dma_begin("sp")
            T.dma(g, st[:], x_d[s, t * 128:(t + 1) * 128, :], [V(None, *sres)], [])
            T.dma_end(g)
            for hf in range(2):
                bank = PS[6 + hf]
                for j in range(4):
                    kc = 4 * hf + j
                    tr(V(bank[:, j * 128:(j + 1) * 128], "ps%d" % (6 + hf)), V(st[:, kc * 128:(kc + 1) * 128], *sres), j == 3)
                dst = V(xT[:, 4 * hf:4 * hf + 4, t * 128:(t + 1) * 128], *["xT.%d.%d" % (4 * hf + j, t // 4) for j in range(4)])
                src = V(bank[:].rearrange("p (a b) -> p a b", a=4), "ps%d" % (6 + hf))
                if hf == 0:
                    act(dst, src, AF.Copy)
                else:
                    T.op("dve", lambda e, d=dst, s_=src: e.tensor_copy(d.ap, s_.ap), [dst], [src])
        for mt in range(0 if DEBUG_NOMEM else 2):
            st = UU[mt % 2]
            sres = ("uu%da" % (mt % 2), "uu%db" % (mt % 2))
            g = T.dma_begin("sp")
            T.dma(g, st[:], mem_d[s, mt * 128:(mt + 1) * 128, :], [V(None, *sres)], [])
            T.dma_end(g)
            for hf in range(2):
                bank = PS[6 + hf]
                for j in range(4):
                    kc = 4 * hf + j
                    tr(V(bank[:, j * 128:(j + 1) * 128], "ps%d" % (6 + hf)), V(st[:, kc * 128:(kc + 1) * 128], *sres), j == 3)
                dst = V(mhT[:, 4 * hf:4 * hf + 4, mt * 128:(mt + 1) * 128], "mh")
                src = V(bank[:].rearrange("p (a b) -> p a b", a=4), "ps%d" % (6 + hf))
                act(dst, src, AF.Copy)
        for kc in range(0 if DEBUG_NOMEM else 8):
            sq = Fv(kc % 2, 0, 128, 0, 256)
            act(sq, V(mhT[:, kc, :], "mh"), AF.Square)
            mm(PSv(7, 0, 128, 0, 256), ONESF, sq, kc == 0, kc == 7)
        if not DEBUG_NOMEM:
            rstd_from(Fv(2, 0, 128, 0, 256), PSv(7, 0, 128, 0, 256), D * EPS)
        for kc in range(0 if DEBUG_NOMEM else 8):
            tt("dve", V(mhT[:, kc, :], "mh"), V(mhT[:, kc, :], "mh"), Fv(2, 0, 128, 0, 256), ALU.mult)

        for l in layers:
            gb = l * GPL
            li = layers.index(l)
            next_gb = layers[li + 1] * GPL if (li + 1 < len(layers) and XNORM_PIPE and DEBUG_KINDS is None) else None
            if (DEBUG_KINDS is None or len(DEBUG_KINDS) > 0) and (li == 0 or not XNORM_PIPE or DEBUG_KINDS is not None):
                xnorm(gb + 0)
            for k in range(17):
                if si + 1 < len(sets):
                    if bg.pending():
                        bg.on_empty.append(lambda k=si + 1: load_set(k))
                    else:
                        load_set(si + 1)
                kind, _, _, u = sets[si]
                dl = (DEFER_TAIL and DEBUG_KINDS is None and si + 1 < len(sets) and sets[si + 1][0] == kind
                      and kind in ("da", "sb"))
                if DEBUG_KINDS is not None and kind not in DEBUG_KINDS:
                    pass
                elif kind == "da":
                    unit_da(si, l, u, dl)
                elif kind == "sb":
                    unit_sb(si, l, u, dl)
                elif kind == "ml":
                    unit_ml(si, l, u)
                elif kind == "mem":
                    unit_mem(si, l)
                else:
                    unit_ff(si, l, u, next_gb)
                si += 1

        for t in range(16):
            st = UU[t % 2]
            sres = ("uu%da" % (t % 2), "uu%db" % (t % 2))
            for hf in range(2):
                bank = PS[6 + hf]
                for j in range(4):
                    kc = 4 * hf + j
                    tr(V(bank[:, j * 128:(j + 1) * 128], "ps%d" % (6 + hf)),
                       V(xT[:, kc, t * 128:(t + 1) * 128], "xT.%d.%d" % (kc, t // 4)), j == 3)
                dst = V(st[:, hf * 512:(hf + 1) * 512], *sres)
                src = V(bank[:], "ps%d" % (6 + hf))
                if hf == 0:
                    act(dst, src, AF.Copy)
                else:
                    T.op("dve", lambda e, d=dst, s_=src: e.tensor_copy(d.ap, s_.ap), [dst], [src])
            g = T.dma_begin("sp")
            T.dma(g, y_d[s, t * 128:(t + 1) * 128, :], st[:], [], [V(None, *sres)])
            T.dma_end(g, is_output=True)
    T.finish()
    print("instructions:", T.ninst, "logical:", T.cnt, "physical incs:", T.pcnt)
    return T.used


def _t5_bucket(rel_np):
    cpu = jax.devices("cpu")[0]
    with jax.default_device(cpu):
        rel = jnp.asarray(rel_np, dtype=jnp.int32)
        nb = 16
        bucket = (rel > 0).astype(jnp.int32) * nb
        n = jnp.abs(rel)
        max_exact = nb // 2
        is_small = n < max_exact
        large = max_exact + (jnp.log(jnp.maximum(n, 1).astype(jnp.float32) / max_exact)
                             / math.log(128 / max_exact) * (nb - max_exact)).astype(jnp.int32)
        large = jnp.minimum(large, nb - 1)
        out = bucket + jnp.where(is_small, n, large)
        return np.asarray(out)


def _const_tables():
    i = np.arange(128)
    ident = np.eye(128, dtype=np.float32)
    ones = np.ones((128, 128), np.float32)
    tri = (i[:, None] > i[None, :]).astype(np.float32)
    bo64 = ((i[:, None] // 64) == (i[None, :] // 64)).astype(np.float32)
    trim = (i[:, None] < i[None, :]).astype(np.float32)
    trip = (i[:, None] <= i[None, :]).astype(np.float32)
    cst = np.concatenate([ident, ones, tri, bo64, trim, trip], axis=1)
    half = 16
    freqs = (10000.0 ** (-np.arange(half, dtype=np.float32) / half)).astype(np.float32)
    pos = np.arange(S, dtype=np.float32)
    ang = pos[None, :] * freqs[:, None]
    cos = np.cos(ang).astype(np.float32)
    sin = np.sin(ang).astype(np.float32)
    rc = np.zeros((128, S), np.float32)
    rs = np.zeros((128, S), np.float32)
    rc[64:80] = cos
    rc[80:96] = cos
    rs[64:80] = -sin
    rs[80:96] = sin
    gs = np.ones((128, GPN), np.float32)
    for l in range(L):
        b = l * GPL
        lam_init = 0.8 - 0.6 * math.exp(-0.3 * l)
        gs[:, b + 0:b + 32] = 32.0
        gs[:, b + 32] = 1.0
        gs[:, b + 33] = 8.0
        gs[:, b + 34] = math.sqrt(128.0) * (1.0 - lam_init)
        gs[:, b + 35] = 8.0
        gs[:, b + 36:b + 38] = 16.0
        gs[:, b + 38] = math.sqrt(128.0)
        gs[:, b + 39:b + 41] = 1.0
        gs[:, b + 41:b + 43] = math.sqrt(96.0)
        gs[:, b + 43] = 8.0
        gs[:, b + 44] = 1.0
        gs[:, b + 45] = 8.0
    kl = np.arange(128)[:, None, None]
    ql = np.arange(128)[None, None, :]
    jj = np.arange(2)[None, :, None]
    bidx = _t5_bucket(kl - ql - 128 * jj)
    return cst, rc, rs, gs, bidx


def _pack_small(inp):
    p = np.arange(128)
    gp = np.zeros((128, GPN), np.float32)

    def perm96(g):
        out = np.array(g[np.minimum(p, 95)], dtype=np.float32)
        out[64:80] = g[80:96]
        out[80:96] = g[64:80]
        return out

    for l in range(L):
        b = l * GPL
        gp[:, b + 0:b + 8] = inp["mix_norm_g"][l].reshape(8, 128).T
        gp[:, b + 8:b + 16] = inp["memx_norm_g"][l].reshape(8, 128).T
        gp[:, b + 16:b + 24] = inp["ffn_norm_g"][l].reshape(8, 128).T
        gp[:, b + 24:b + 32] = inp["mem_norm_g"][l].reshape(8, 128).T
        gp[:, b + 32] = inp["da_q_norm_g"][l][p % 64]
        gp[:, b + 33] = inp["da_k_norm_g"][l][p % 64]
        gp[:, b + 34] = inp["da_subln_g"][l]
        gp[:, b + 35] = inp["sb_out_g"][l][p % 64]
        gp[:, b + 36:b + 38] = inp["mla_cq_norm_g"][l].reshape(2, 128).T
        gp[:, b + 38] = inp["mla_ckv_norm_g"][l]
        gp[:, b + 39] = inp["mla_q_norm_g"][l][np.minimum(p, 95)]
        gp[:, b + 40] = perm96(inp["mla_q_norm_g"][l])
        gp[:, b + 41] = inp["mla_k_norm_g"][l][np.minimum(p, 95)]
        gp[:, b + 42] = perm96(inp["mla_k_norm_g"][l])
        gp[:, b + 43] = inp["mla_out_g"][l][p % 64]
        gp[:, b + 44] = inp["mem_q_norm_g"][l][p % 64]
        gp[:, b + 45] = inp["mem_k_norm_g"][l][p % 64]
    gp[:, GPL * L:GPL * L + 8] = np.broadcast_to(inp["rel_bias"][15][None, :], (128, 8))
    lamb = np.ascontiguousarray(np.broadcast_to(inp["da_lambda"].reshape(1, L * 256), (128, L * 256)), dtype=np.float32)
    return gp, lamb


_PROGRAMS = {}


def _get_program(nseq, layers):
    key = (nseq, tuple(layers))
    if key not in _PROGRAMS:
        _PROGRAMS[key] = build_program(nseq, list(layers))
    return _PROGRAMS[key]


def _shared_inputs(inp):
    cst, rc, rs, gs, bidx = _const_tables()
    gp, lamb = _pack_small(inp)
    rb = np.asarray(inp["rel_bias"], np.float32)
    bt = np.ascontiguousarray(np.transpose(rb[bidx], (0, 3, 1, 2)))
    shared = {
        "gp": gp, "gscale": gs, "lamb": lamb, "bt": bt, "cst": cst, "ropec": rc, "ropes": rs,
    }
    for k in ("w_in", "w_mla_uq", "w_mla_ukv", "w_out", "w_mem_q", "w_mem_kv", "w_mem_o", "w_ff1", "w_ff2"):
        shared[k] = np.ascontiguousarray(inp[k], dtype=np.float32)
    return shared


FUSED = True
PIPELINE = True
BG_FRONTLOAD = 1.3
ATTACH_WAIT = True
BF16_STATS = True
DEFER_TAIL = True
PRUNE_INCS = True
DEN_ON_PE = True
DUMMY_MM = 0
XNORM_PIPE = True
DEBUG_KINDS = None
DEBUG_NOSETS = False
DEBUG_SETKINDS = None
DEBUG_NOLAM = False
DEBUG_NOMEM = False


def kernel(**inp):
    x = np.ascontiguousarray(inp["x"], dtype=np.float32)
    mem = np.ascontiguousarray(inp["mem"], dtype=np.float32)
    shared = _shared_inputs(inp)
    xs = [x[c * NSEQ:(c + 1) * NSEQ] for c in range(NCORES)]
    ms = [mem[c * NSEQ:(c + 1) * NSEQ] for c in range(NCORES)]
    launches = [list(range(L))] if FUSED else [[l] for l in range(L)]
    for layers in launches:
        nc = _get_program(NSEQ, layers)
        in_maps = []
        for c in range(NCORES):
            m = dict(shared)
            m["x"] = xs[c]
            m["mem"] = ms[c]
            in_maps.append(m)
        res = run_bass_kernel_spmd(nc, in_maps, core_ids=list(range(NCORES)))
        xs = [np.asarray(res.results[c]["y"], dtype=np.float32) for c in range(NCORES)]
    return np.concatenate(xs, axis=0)
```
